# Optimizing a Trainium2 kernel written in Bass

```python
import jax, jax.numpy as jnp
from jax import lax
import numpy as np

D_MODEL = 1024
BATCH = 1
SEQ = 16384
DEPTH = 1
DEC_BATCH = 32
DEC_SEQ = 2048
PAST_LEN = 128

GRID_W = 64
M_HEADS = 4
M_HEAD_DIM = 256
M_WIDTH = M_HEADS * M_HEAD_DIM
M_CHUNK = 64
CONV_W = 5
N_GATES = 4 * M_HEADS
NA_HEADS = 16
NA_HEAD_DIM = 64
NA_WIDTH = NA_HEADS * NA_HEAD_DIM
NA_MAX_KH = 8
NA_KW = 16
NA_QCB = 16
NA_KCB = 32
SPLITS = (M_WIDTH, M_WIDTH, M_WIDTH, M_WIDTH, M_WIDTH, N_GATES, NA_WIDTH, NA_WIDTH, NA_WIDTH, NA_WIDTH, D_MODEL, D_MODEL)
D_IN = 5 * M_WIDTH + N_GATES + 4 * NA_WIDTH + 2 * D_MODEL
EPS = 1e-6
NEG = -1e30

kernel_name = 'hybrid_mlstm_natten_encoder'


def _rmsnorm(x, w):
    x32 = x.astype(jnp.float32)
    y = x32 * lax.rsqrt(jnp.mean(x32 * x32, axis=-1, keepdims=True) + EPS)
    return (y * w.astype(jnp.float32)).astype(x.dtype)


def _centred_conv(u, w, b):
    T = u.shape[1]
    pad = CONV_W // 2
    up = jnp.pad(u, ((0, 0), (pad, pad), (0, 0)))
    out = b
    for i in range(CONV_W):
        out = out + up[:, i:i + T] * w[i]
    return out


def _mlstm_scan(q, k, v, li, lf):
    B, H, T, dk = q.shape
    dv = v.shape[-1]
    nc = T // M_CHUNK

    def chunks(a):
        return jnp.moveaxis(a.reshape((B, H, nc, M_CHUNK) + a.shape[3:]), 2, 0)

    qc, kc, vc, lic = chunks(q), chunks(k), chunks(v), chunks(li)
    bc = jnp.cumsum(chunks(lf), axis=-1)
    tril = jnp.tril(jnp.ones((M_CHUNK, M_CHUNK), dtype=bool))

    def step(carry, xs):
        S, n, m = carry
        q_, k_, v_, li_, b_ = xs
        dmat = jnp.where(tril, b_[..., :, None] - b_[..., None, :] + li_[..., None, :], -jnp.inf)
        inter = b_ + m[..., None]
        mj = jnp.maximum(inter, jnp.max(dmat, axis=-1))
        w_int = jnp.exp(inter - mj)
        p = jnp.einsum('bhld,bhsd->bhls', q_, k_) * jnp.exp(dmat - mj[..., None])
        num = w_int[..., None] * jnp.einsum('bhld,bhde->bhle', q_, S) + jnp.einsum('bhls,bhse->bhle', p, v_)
        den = w_int * jnp.einsum('bhld,bhd->bhl', q_, n) + jnp.sum(p, axis=-1)
        h = num / jnp.maximum(jnp.abs(den), jnp.exp(-mj))[..., None]
        b_last = b_[..., -1]
        g = b_last[..., None] - b_ + li_
        m_new = jnp.maximum(b_last + m, jnp.max(g, axis=-1))
        a = jnp.exp(b_last + m - m_new)
        kw = k_ * jnp.exp(g - m_new[..., None])[..., None]
        S_new = a[..., None, None] * S + jnp.einsum('bhld,bhle->bhde', kw, v_)
        n_new = a[..., None] * n + jnp.sum(kw, axis=2)
        return (S_new, n_new, m_new), h

    init = (jnp.zeros((B, H, dk, dv), jnp.float32), jnp.zeros((B, H, dk), jnp.float32),
            jnp.zeros((B, H), jnp.float32))
    _, h = lax.scan(step, init, (qc, kc, vc, lic, bc))
    return jnp.moveaxis(h, 0, 2).reshape(B, H, T, dv)


def _mlstm_branch(q_pre, k_pre, v_in, o_pre, gate_pre, b_gate, conv_w, conv_b, mh_w):
    B, T, _ = q_pre.shape
    qk = jax.nn.silu(_centred_conv(jnp.concatenate([q_pre, k_pre], axis=-1), conv_w, conv_b))
    q, k = jnp.split(qk, 2, axis=-1)

    def heads(a):
        return a.astype(jnp.float32).reshape(B, T, M_HEADS, M_HEAD_DIM).transpose(0, 2, 1, 3)

    q, k, v = heads(q), heads(k) * (M_HEAD_DIM ** -0.5), heads(v_in)
    g = (gate_pre.astype(jnp.float32) + b_gate.astype(jnp.float32)).transpose(0, 2, 1)
    ig_f, fg_f, ig_b, fg_b = jnp.split(g, 4, axis=1)
    h_f = _mlstm_scan(q, k, v, ig_f, jax.nn.log_sigmoid(fg_f))

    def flip(a):
        return jnp.flip(a, axis=2)

    h_b = flip(_mlstm_scan(flip(q), flip(k), flip(v), flip(ig_b), flip(jax.nn.log_sigmoid(fg_b))))
    h = (h_f + h_b).transpose(0, 2, 1, 3)
    h = jax.nn.sigmoid(o_pre.astype(jnp.float32)).reshape(B, T, M_HEADS, M_HEAD_DIM) * h
    h = h * lax.rsqrt(jnp.mean(h * h, axis=-1, keepdims=True) + EPS) \
        * mh_w.astype(jnp.float32).reshape(M_HEADS, M_HEAD_DIM)
    return h.reshape(B, T, M_WIDTH).astype(q_pre.dtype)


def _na_col_tables():
    c = np.arange(GRID_W)
    cs = np.clip(c - NA_KW // 2, 0, GRID_W - NA_KW)
    ncb = GRID_W // NA_QCB
    c0 = np.arange(ncb) * NA_QCB
    kb = np.clip(c0 - NA_KW // 2, 0, GRID_W - NA_KCB)
    kcol = kb[:, None] + np.arange(NA_KCB)[None, :]
    qcol = c.reshape(ncb, NA_QCB)
    dc = kcol[:, None, :] - qcol[:, :, None]
    qs = cs.reshape(ncb, NA_QCB)[:, :, None]
    valid = (kcol[:, None, :] >= qs) & (kcol[:, None, :] < qs + NA_KW)
    return kcol, dc, valid


def _neighbourhood_attention(q, k, v, rpb):
    B, T, _ = q.shape
    rows = T // GRID_W
    kh = min(NA_MAX_KH, rows)
    ncb = GRID_W // NA_QCB
    kcol, dc, valid = _na_col_tables()
    qg = q.reshape(B, rows, ncb, NA_QCB, NA_HEADS, NA_HEAD_DIM)
    kg = k.reshape(B, rows, GRID_W, NA_HEADS, NA_HEAD_DIM)
    vg = v.reshape(B, rows, GRID_W, NA_HEADS, NA_HEAD_DIM)
    rpb_c = rpb[:, :, np.clip(dc + NA_KW - 1, 0, 2 * NA_KW - 2)]
    valid_b = jnp.asarray(valid)[None, None, :, :, None, :]
    scale = NA_HEAD_DIM ** -0.5

    def row(r):
        rs = jnp.clip(r - kh // 2, 0, rows - kh)
        k_blk = lax.dynamic_slice_in_dim(kg, rs, kh, axis=1)[:, :, kcol]
        v_blk = lax.dynamic_slice_in_dim(vg, rs, kh, axis=1)[:, :, kcol]
        q_r = lax.dynamic_index_in_dim(qg, r, axis=1, keepdims=False)
        s = jnp.einsum('bnqhd,bknjhd->bhnqkj', q_r, k_blk).astype(jnp.float32) * scale
        dr = rs + jnp.arange(kh) - r + (NA_MAX_KH - 1)
        bias = jnp.take(rpb_c, dr, axis=1).transpose(0, 2, 3, 1, 4)
        s = jnp.where(valid_b, s + bias.astype(jnp.float32)[None], NEG)
        shp = s.shape
        p = jax.nn.softmax(s.reshape(shp[:4] + (kh * NA_KCB,)), axis=-1).reshape(shp).astype(v.dtype)
        o = jnp.einsum('bhnqkj,bknjhd->bnqhd', p, v_blk)
        return o.reshape(B, GRID_W, NA_WIDTH)

    out = lax.map(row, jnp.arange(rows))
    return out.transpose(1, 0, 2, 3).reshape(B, T, NA_WIDTH)


def _layer(x, norm_w, w_in, b_gate, conv_w, conv_b, mh_norm_w, rpb, w_down_a, w_down_b, w_out):
    xn = _rmsnorm(x, norm_w)
    idx = np.cumsum(SPLITS)[:-1]
    m_q, m_k, m_v, m_o, m_z, m_g, n_q, n_k, n_v, n_z, g_a, g_b = [xn @ w for w in jnp.split(w_in, idx, axis=1)]
    y_a = _mlstm_branch(m_q, m_k, m_v, m_o, m_g, b_gate, conv_w, conv_b, mh_norm_w) * jax.nn.silu(m_z)
    y_b = _neighbourhood_attention(n_q, n_k, n_v, rpb) * jax.nn.silu(n_z)
    merged = jax.nn.sigmoid(g_a) * (y_a @ w_down_a) + jax.nn.sigmoid(g_b) * (y_b @ w_down_b)
    return x + merged @ w_out


def _trunk(x, norm_w, w_in, b_gate, conv_w, conv_b, mh_norm_w, rpb, w_down_a, w_down_b, w_out, final_norm_w):
    for l in range(DEPTH):
        x = _layer(x, norm_w[l], w_in[l], b_gate[l], conv_w[l], conv_b[l], mh_norm_w[l], rpb[l],
                   w_down_a[l], w_down_b[l], w_out[l])
    return _rmsnorm(x, final_norm_w)


def setup_inputs(seed: int = 0) -> dict:
    key = jax.random.key(seed)
    ks = jax.random.split(key, 13)
    nrm = jax.random.normal
    fb = jnp.linspace(3.0, 6.0, M_HEADS)
    zb = jnp.zeros((M_HEADS,), jnp.float32)
    gate_base = jnp.concatenate([zb, fb, zb, fb])
    return {
        'x_prompt': nrm(ks[0], (BATCH, SEQ, D_MODEL), jnp.float32),
        'x_sample': nrm(ks[1], (DEC_BATCH, DEC_SEQ, D_MODEL), jnp.float32),
        'norm_w': 1.0 + 0.02 * nrm(ks[2], (DEPTH, D_MODEL), jnp.float32),
        'w_in': nrm(ks[3], (DEPTH, D_MODEL, D_IN), jnp.float32) * D_MODEL ** -0.5,
        'b_gate': gate_base[None, :] + 0.1 * nrm(ks[4], (DEPTH, N_GATES), jnp.float32),
        'conv_w': nrm(ks[5], (DEPTH, CONV_W, 2 * M_WIDTH), jnp.float32) * CONV_W ** -0.5,
        'conv_b': 0.01 * nrm(ks[6], (DEPTH, 2 * M_WIDTH), jnp.float32),
        'mh_norm_w': 1.0 + 0.02 * nrm(ks[7], (DEPTH, M_WIDTH), jnp.float32),
        'rpb': 0.1 * nrm(ks[8], (DEPTH, NA_HEADS, 2 * NA_MAX_KH - 1, 2 * NA_KW - 1), jnp.float32),
        'w_down_a': nrm(ks[9], (DEPTH, M_WIDTH, D_MODEL), jnp.float32) * M_WIDTH ** -0.5,
        'w_down_b': nrm(ks[10], (DEPTH, NA_WIDTH, D_MODEL), jnp.float32) * NA_WIDTH ** -0.5,
        'w_out': nrm(ks[11], (DEPTH, D_MODEL, D_MODEL), jnp.float32) * D_MODEL ** -0.5,
        'final_norm_w': 1.0 + 0.02 * nrm(ks[12], (D_MODEL,), jnp.float32),
    }


def reference(x_prompt, x_sample, norm_w, w_in, b_gate, conv_w, conv_b, mh_norm_w, rpb,
              w_down_a, w_down_b, w_out, final_norm_w):
    y_prompt = _trunk(x_prompt, norm_w, w_in, b_gate, conv_w, conv_b, mh_norm_w, rpb,
                      w_down_a, w_down_b, w_out, final_norm_w)
    y_sample = _trunk(x_sample, norm_w, w_in, b_gate, conv_w, conv_b, mh_norm_w, rpb,
                      w_down_a, w_down_b, w_out, final_norm_w)
    return (y_prompt, y_sample)
```

```python
import contextlib
import numpy as np
import concourse.bass as bass
import concourse.mybir as mybir
from concourse.bass_utils import run_bass_kernel_spmd

F32 = mybir.dt.float32
BF16 = mybir.dt.bfloat16
AF = mybir.ActivationFunctionType
ALU = mybir.AluOpType

D = 1024
D_IN = 11280
NCORES = 8
NUNITS = 5
UT = 2048
XT = 21
XROWS = XT * 128
EPS = 1e-6
NEGB = -200.0
MQ0, MK0, MV0, MO0, MZ0, MG0 = 0, 1024, 2048, 3072, 4096, 5120
NQ0, NK0, NV0, NZ0, GA0, GB0 = 5136, 6160, 7184, 8208, 9232, 10256
JB = (0, 1, 14, 15)


import heapq


class Buf:
    __slots__ = ("name", "w", "r", "dsem", "dn", "last_dma")

    def __init__(self, name):
        self.name = name
        self.w = None
        self.r = []
        self.dsem = None
        self.dn = 0
        self.last_dma = None


class Eng:
    def __init__(self, name, eng, sem):
        self.name, self.eng, self.sem = name, eng, sem
        self.count = 0
        self.waited = {}


class Op:
    __slots__ = ("eng", "calls", "preds", "dur", "idx", "kind", "track", "token", "end", "nsucc", "succs", "dma",
                 "prio")

    def __init__(self):
        self.token = None
        self.end = 0.0
        self.succs = []


class _Dummy:
    def then_inc(self, *a, **k):
        return self


class Rec:
    def __init__(self):
        self.calls = []

    def __getattr__(self, name):
        def f(*args, **kwargs):
            self.calls.append((name, args, kwargs))
            return _Dummy()
        return f


def _fsize(ap):
    n = 1
    for s in list(ap.shape)[1:]:
        n *= int(s)
    return n


def _cost(ename, calls):
    t = 0.0
    for name, args, kw in calls:
        if name == "matmul":
            t += 0.03 + _fsize(kw["rhs"]) / 2000.0
        elif name == "transpose":
            t += 0.1
        elif name == "activation":
            t += 0.22 + _fsize(kw["in_"]) / 1000.0
        elif name == "reciprocal":
            t += 0.15 + 8 * _fsize(kw["in_"]) / 960.0
        elif name == "memset":
            t += 0.1 + _fsize(args[0]) / 960.0
        else:
            ap = kw.get("in0", kw.get("in_", None))
            n = _fsize(ap) if ap is not None else 64
            t += (0.5 + n / 500.0) if ename == "pool" else (0.13 + n / 960.0)
    return t


class Trk:
    HOP = 0.05

    def __init__(self, nc, es):
        self.nc = nc
        self.es = es
        self.E = {}
        self.pending = []
        self.nidx = 0
        self.prio = 1

    def add_engine(self, name, eng):
        sem = self.es.enter_context(self.nc.semaphore("e_" + name))
        self.E[name] = Eng(name, eng, sem)
        return self.E[name]

    def dma_buf(self, name):
        b = Buf(name)
        b.dsem = self.es.enter_context(self.nc.semaphore("d_" + name))
        return b

    def _link(self, op, reads, writes):
        preds = set()
        for b in reads:
            if b.w is not None:
                preds.add(b.w)
        for b in writes:
            if b.w is not None:
                preds.add(b.w)
            for r_ in b.r:
                preds.add(r_)
        preds.discard(op)
        op.preds = list(preds)
        for b in writes:
            b.w = op
            b.r = []
        for b in reads:
            b.r.append(op)
        op.idx = self.nidx
        op.prio = self.prio
        self.nidx += 1
        self.pending.append(op)

    def op(self, ename, fn, reads=(), writes=()):
        rec = Rec()
        fn(rec)
        o = Op()
        o.eng, o.calls, o.kind, o.track = ename, rec.calls, "c", None
        o.dur = _cost(ename, rec.calls)
        self._link(o, reads, writes)
        return o

    def dma(self, qname, out, in_, reads=(), writes=(), track=None):
        o = Op()
        o.eng, o.calls, o.kind, o.track = qname, None, "d", track
        o.dma = (out, in_)
        nbytes = _fsize(out) * 128 * 4
        o.dur = 2.0 + nbytes / 300000.0
        self._link(o, reads, writes)
        if track.last_dma is not None and track.last_dma not in o.preds:
            o.preds.append(track.last_dma)
        track.last_dma = o
        return o

    def flush(self):
        ops = self.pending
        self.pending = []
        if not ops:
            return
        inseg = set(id(o) for o in ops)
        for o in ops:
            o.nsucc = 0
            o.succs = []
        npend = {}
        for o in ops:
            c = 0
            for p in o.preds:
                if id(p) in inseg:
                    p.succs.append(o)
                    c += 1
            npend[id(o)] = c
        ready_t = {}
        free = {e: 0.0 for e in self.E}
        future = {e: [] for e in self.E}
        avail = {e: [] for e in self.E}
        order = {e: [] for e in self.E}

        def make_ready(o):
            t = 0.0
            for p in o.preds:
                if id(p) in inseg:
                    lat = 0.0 if (p.eng == o.eng) else self.HOP
                    t = max(t, p.end + lat)
            heapq.heappush(future[o.eng], (t, o.idx, o))

        for o in ops:
            if npend[id(o)] == 0:
                make_ready(o)
        nleft = len(ops)
        while nleft:
            best = None
            for e in self.E:
                fu, av = future[e], avail[e]
                while fu and fu[0][0] <= free[e]:
                    _, i_, o_ = heapq.heappop(fu)
                    heapq.heappush(av, (o_.prio, i_, o_))
                if av:
                    st = free[e]
                elif fu:
                    st = fu[0][0]
                else:
                    continue
                if best is None or st < best[0]:
                    best = (st, e)
            st, e = best
            if avail[e]:
                _, _, o = heapq.heappop(avail[e])
            else:
                _, _, o = heapq.heappop(future[e])
            if o.kind == "d":
                free[e] = st + 0.08
                o.end = st + o.dur
            else:
                free[e] = st + o.dur
                o.end = free[e]
            order[e].append(o)
            nleft -= 1
            for s_ in o.succs:
                npend[id(s_)] -= 1
                if npend[id(s_)] == 0:
                    make_ready(s_)
        for e, lst in order.items():
            E = self.E[e]
            c = E.count
            for o in lst:
                if o.kind == "c":
                    c += 1
                    o.token = (E.name, E.sem, c)
                else:
                    o.track.dn += 1
                    o.token = ("d_" + o.track.name, o.track.dsem, 16 * o.track.dn)
        for e, lst in order.items():
            E = self.E[e]
            for o in lst:
                deps = {}
                for p in o.preds:
                    k, sem, val = p.token
                    if k == "pe" and e == "pe":
                        continue
                    if k not in deps or deps[k][1] < val:
                        deps[k] = (sem, val)
                for k, (sem, val) in deps.items():
                    if E.waited.get(k, 0) < val:
                        E.eng.wait_ge(sem, val)
                        E.waited[k] = val
                if o.kind == "c":
                    ins = None
                    for name, args, kw in o.calls:
                        ins = getattr(E.eng, name)(*args, **kw)
                    E.count += 1
                    ins.then_inc(E.sem, 1)
                else:
                    ins = E.eng.dma_start(out=o.dma[0], in_=o.dma[1])
                    ins.then_inc(o.track.dsem, 16)
                o.calls = None
                o.dma = None

    def wait_all(self, ename, bufs):
        self.flush()
        E = self.E[ename]
        deps = {}
        for b in bufs:
            for o in ([b.w] if b.w is not None else []) + list(b.r):
                k, sem, val = o.token
                if k not in deps or deps[k][1] < val:
                    deps[k] = (sem, val)
        for k, (sem, val) in deps.items():
            if E.waited.get(k, 0) < val:
                E.eng.wait_ge(sem, val)
                E.waited[k] = val


def build_program(nunits=NUNITS, debug=False, prepass=True):
    nc = bass.Bass("TRN2", target_bir_lowering=False)
    es = contextlib.ExitStack()
    with es:
        tk = Trk(nc, es)
        tk.add_engine("pe", nc.tensor)
        tk.add_engine("act", nc.scalar)
        tk.add_engine("dve", nc.vector)
        tk.add_engine("pool", nc.gpsimd)
        tk.add_engine("sp", nc.sync)

        def dram(name, shape, dt, kind):
            return nc.dram_tensor(name, list(shape), dt, kind=kind).ap()

        xu = dram("xu", [nunits, XROWS, D], F32, "ExternalInput")
        w_in = dram("w_in", [D, D_IN], F32, "ExternalInput")
        w_da = dram("w_da", [D, D], F32, "ExternalInput")
        w_db = dram("w_db", [D, D], F32, "ExternalInput")
        w_out = dram("w_out", [D, D], F32, "ExternalInput")
        c_normT = dram("c_normT", [128, 8], F32, "ExternalInput")
        c_normb = dram("c_normb", [128, D], F32, "ExternalInput")
        c_cw = dram("c_cw", [128, 16 * 5], F32, "ExternalInput")
        c_cb = dram("c_cb", [128, 16], F32, "ExternalInput")
        c_bg = dram("c_bg", [128, 16], F32, "ExternalInput")
        c_mhw = dram("c_mhw", [128, D], F32, "ExternalInput")
        c_fnw = dram("c_fnw", [128, D], F32, "ExternalInput")
        c_mask = dram("c_mask", [128, 3 * 128], F32, "ExternalInput")
        c_ident = dram("c_ident", [128, 128], F32, "ExternalInput")
        c_nai = dram("c_nai", [128, 16 * 5 * 128], F32, "ExternalInput")
        c_nab = dram("c_nab", [2 * 4 * 8, 128, 2 * 5 * 128], F32, "ExternalInput")
        if prepass:
            xpre = dram("xpre", [7, 17 * 128, D], F32, "ExternalInput")
            c_pgw = dram("c_pgw", [7, D, 8], F32, "ExternalInput")
            c_pgb = dram("c_pgb", [128, 56], F32, "ExternalInput")
            c_pcw = dram("c_pcw", [128, 7 * 40], F32, "ExternalInput")
            c_pflags = dram("c_pflags", [128, 16], F32, "ExternalInput")
            sc_st = dram("sc_st", [2, 4, 128, 514], F32, "Internal")
        yu = dram("yu", [nunits, UT, D], F32, "ExternalOutput")
        sc_ya = dram("sc_ya", [nunits, 128, 8, UT], BF16, "Internal")
        sc_yb = dram("sc_yb", [nunits, 128, 8, UT], BF16, "Internal")
        dbg = {}
        if debug:
            dbg["xnT"] = dram("dbg_xnT", [128, 1024], F32, "ExternalOutput")
            dbg["gates"] = dram("dbg_gates", [128, 384], F32, "ExternalOutput")

        def sb(name, shape, dt):
            return es.enter_context(nc.sbuf_tensor(name, list(shape), dt))

        def psum(name, shape, dt):
            return es.enter_context(nc.psum_tensor(name, list(shape), dt))

        NPF = 6
        psf = [psum("psf%d" % i, [128, 512], F32) for i in range(NPF)]
        psf_b = [Buf("psf%d" % i) for i in range(NPF)]
        psb = [psum("psb%d" % i, [128, 1024], BF16) for i in range(2)]
        psb_b = [Buf("psb%d" % i) for i in range(2)]
        rr = {"f": 0, "b": 0, "n": NPF, "g": 0, "split": False}

        def nextf():
            i = rr["f"] % rr["n"]
            rr["f"] += 1
            return psf[i], psf_b[i]

        def nextg():
            if not rr.get("split"):
                return nextf()
            i = 4 + rr["g"] % 2
            rr["g"] += 1
            return psf[i], psf_b[i]

        def nextb():
            i = rr["b"] % 2
            rr["b"] += 1
            return psb[i], psb_b[i]

        cst = tk.dma_buf("cst")
        normT = sb("normT", [128, 8], F32)
        cw = sb("cw", [128, 80], F32)
        cb = sb("cb", [128, 16], F32)
        bg = sb("bg", [128, 16], F32)
        mask = sb("mask", [128, 384], F32)
        normb = sb("normb", [128, D], F32)
        ident_f = sb("ident_f", [128, 128], F32)
        ident = sb("ident", [128, 128], BF16)
        ident_b = Buf("ident")
        col = sb("col", [128, 4], F32)
        col_b = Buf("col")
        for dst, src in ((normT, c_normT), (cw, c_cw), (cb, c_cb), (bg, c_bg), (mask, c_mask), (ident_f, c_ident),
                         (normb, c_normb)):
            tk.dma("sp", dst[:], src[:, :], writes=[cst], track=cst)
        tk.op("dve", lambda g: g.tensor_copy(out=ident[:], in_=ident_f[:]), reads=[cst], writes=[ident_b])
        tk.op("dve", lambda g: g.memset(col[:, 0:1], 1.0), writes=[col_b])
        tk.op("dve", lambda g: g.memset(col[:, 1:2], -0.5), writes=[col_b])
        tk.op("dve", lambda g: g.memset(col[:, 2:3], -float(np.log(16.0))), writes=[col_b])
        tk.op("dve", lambda g: g.memset(col[:, 3:4], 0.0), writes=[col_b])
        cbh = sb("cbh", [128, 16], F32)
        cbh_b = Buf("cbh")
        tk.op("dve", lambda g: g.tensor_scalar(out=cbh[:], in0=cb[:], scalar1=0.5, scalar2=None, op0=ALU.mult),
              reads=[cst], writes=[cbh_b])
        tht = [sb("tht%d" % i, [128, 512], F32) for i in range(2)]
        tht_b = [Buf("tht%d" % i) for i in range(2)]
        mask_f = mask[:, 0:128]
        mask_b = mask[:, 128:256]
        ones32 = mask[:, 256:384]

        xnT = sb("xnT", [128, 8, XROWS], BF16)
        xnT_b = [Buf("xnT%d" % e) for e in range(XT)]
        xs = [sb("xs%d" % i, [128, D], F32) for i in range(2)]
        xs_b = [tk.dma_buf("xs%d" % i) for i in range(2)]
        xb = [sb("xb%d" % i, [128, D], BF16) for i in range(2)]
        xb_b = [Buf("xb%d" % i) for i in range(2)]
        junk = sb("junk", [128, D], BF16)
        junk_b = Buf("junk")
        sm = [sb("sm%d" % i, [128, 8], F32) for i in range(4)]
        sm_b = [Buf("sm%d" % i) for i in range(4)]
        smr = {"i": 0}

        def nextsm():
            i = smr["i"] % 4
            smr["i"] += 1
            return sm[i], sm_b[i]

        ev = sb("ev", [128, 16, 2, 4], F32)
        rsv = sb("rsv", [128, 16, 2, 4], F32)
        av = sb("av", [128, 16, 2, 4], F32)
        nrsv = sb("nrsv", [128, 16, 2, 4], F32)
        gate_b = [Buf("gate%d" % t) for t in range(16)]
        gtmp = [sb("gtmp%d" % i, [128, 2, 8], F32) for i in range(2)]
        gtmp2 = [sb("gtmpb%d" % i, [128, 3, 2, 4], F32) for i in range(2)]
        gtmp_b = [Buf("gtmp%d" % i) for i in range(2)]

        NW = 6
        wr = [sb("wr%d" % i, [128, 8, 256], BF16) for i in range(NW)]
        wr_b = [tk.dma_buf("wr%d" % i) for i in range(NW)]
        wrr = {"i": 0}

        def load_w(wap, c0, ncols=256):
            i = wrr["i"] % NW
            wrr["i"] += 1
            src = wap[:, c0:c0 + ncols].rearrange("(f p) c -> p f c", p=128)
            tk.dma("pool", wr[i][:, :, 0:ncols], src, writes=[wr_b[i]], track=wr_b[i])
            return wr[i], wr_b[i]

        yst = [sb("yst%d" % i, [128, 2, 128], BF16) for i in range(4)]
        yst_b = [tk.dma_buf("yst%d" % i) for i in range(4)]
        ystr = {"i": 0}
        scya_b = [[Buf("scya%d_%d" % (u, t)) for t in range(16)] for u in range(nunits)]
        scyb_b = [[Buf("scyb%d_%d" % (u, t)) for t in range(16)] for u in range(nunits)]
        nstage_b = [tk.dma_buf("nstage%d" % i) for i in range(2)]
        ytl_b = [tk.dma_buf("ytl%d" % i) for i in range(2)]
        res_b = [tk.dma_buf("res%d" % i) for i in range(2)]
        pcst_b = tk.dma_buf("pcst")
        dbg_b = []

        def barrier(dbufs=()):
            tk.flush()
            for e in tk.E.values():
                for f_ in tk.E.values():
                    if f_ is e or f_.count == 0:
                        continue
                    if e.waited.get(f_.name, 0) < f_.count:
                        e.eng.wait_ge(f_.sem, f_.count)
                        e.waited[f_.name] = f_.count
                for b_ in dbufs:
                    k = "d_" + b_.name
                    if b_.dn and e.waited.get(k, 0) < 16 * b_.dn:
                        e.eng.wait_ge(b_.dsem, 16 * b_.dn)
                        e.waited[k] = 16 * b_.dn

        def mm_group(writes, reads, fn):
            return tk.op("pe", fn, reads=reads, writes=writes)

        uid = {"n": 0}

        def phase_alloc(ph):
            uid["n"] += 1
            tag = "_p%d" % uid["n"]

            def sbp(name, shape, dt):
                return ph.enter_context(nc.sbuf_tensor(name + tag, list(shape), dt))
            return sbp

        def run(g_):
            for _ in g_:
                pass

        def chain_g(*gens):
            for g_ in gens:
                yield from g_

        def adv(g_, k):
            if g_ is None:
                return
            for _ in range(k):
                try:
                    next(g_)
                except StopIteration:
                    return

        def stage0(src_fn, tiles):
            for st_, e in tiles:
                s = e % 2
                tk.dma("sp", xs[s][:], src_fn(st_), writes=[xs_b[s]], track=xs_b[s])
                c1, c1b = nextsm()
                tk.op("act", lambda g: g.activation(out=junk[:], in_=xs[s][:], func=AF.Square, accum_out=c1[:, 0:1]),
                      reads=[xs_b[s]], writes=[junk_b, c1b])
                tk.op("dve", lambda g: g.tensor_scalar(out=c1[:, 1:2], in0=c1[:, 0:1], scalar1=1.0 / D, scalar2=EPS,
                                                       op0=ALU.mult, op1=ALU.add), reads=[c1b], writes=[c1b])
                tk.op("pool", lambda g: g.tensor_tensor(out=c1[:, 2:3], in0=c1[:, 1:2], in1=col[:, 1:2], op=ALU.pow),
                      reads=[c1b, col_b], writes=[c1b])
                tk.op("dve", lambda g: g.scalar_tensor_tensor(out=xb[s][:], in0=xs[s][:], scalar=c1[:, 2:3],
                                                              in1=normb[:, :], op0=ALU.mult, op1=ALU.mult),
                      reads=[xs_b[s], c1b, cst], writes=[xb_b[s]])
                pb, pbb = nextb()

                def f(g):
                    last = None
                    for fc in range(8):
                        last = g.transpose(out=pb[:, fc * 128:(fc + 1) * 128], in_=xb[s][:, fc * 128:(fc + 1) * 128],
                                           identity=ident[:])
                    return last
                mm_group([pbb], [xb_b[s], ident_b], f)
                if e % 2 == 0:
                    tk.op("act", lambda g: g.activation(out=xnT[:, :, e * 128:(e + 1) * 128],
                                                        in_=pb[:, 0:1024].rearrange("p (f t) -> p f t", f=8),
                                                        func=AF.Copy), reads=[pbb], writes=[xnT_b[e]])
                else:
                    tk.op("dve", lambda g: g.tensor_copy(out=xnT[:, :, e * 128:(e + 1) * 128],
                                                         in_=pb[:, 0:1024].rearrange("p (f t) -> p f t", f=8)),
                          reads=[pbb], writes=[xnT_b[e]])
                yield

        def qk_proj(B, h, qk, cwt, cwdeps, ch_off, parts=("mm", "conv"), silu=False):
            dstT, dst_b = (B['qT'], B['qT_b']) if qk == 0 else (B['kT'], B['kT_b'])
            if "mm" in parts:
                wq, wqb = load_w(w_in, (MQ0 if qk == 0 else MK0) + h * 256)
                for m in range(2):
                    ch = qk * 8 + h * 2 + m
                    ch_ = ch - ch_off
                    ps_ = (qk * 2 + m) % 2
                    pr, prb = B['pre'][m], B['pre_b'][m]
                    for tb in range(4):
                        pp, ppb = nextg()

                        def f(g):
                            last = None
                            for fc in range(8):
                                last = g.matmul(pp[:, 0:512], lhsT=wq[:, fc, m * 128:(m + 1) * 128],
                                                rhs=xnT[:, fc, 256 + tb * 512:256 + (tb + 1) * 512],
                                                start=(fc == 0), stop=(fc == 7))
                            return last
                        mm_group([ppb], [wqb] + xnT_b[2 + tb * 4:2 + tb * 4 + 4], f)
                        tk.op("act", lambda g: g.activation(out=pr[:, 2 + tb * 512:2 + (tb + 1) * 512],
                                                            in_=pp[:, 0:512], func=AF.Identity, scale=0.5),
                              reads=[ppb], writes=[prb])
                        yield
                    pp, ppb = nextg()

                    def f(g):
                        last = None
                        for fc in range(8):
                            last = g.matmul(pp[:, 0:4], lhsT=wq[:, fc, m * 128:(m + 1) * 128],
                                            rhs=xnT[:, fc, 2560:2564], start=(fc == 0), stop=(fc == 7))
                        return last
                    mm_group([ppb], [wqb, xnT_b[20]], f)
                    tk.op("dve", lambda g: g.tensor_scalar(out=pr[:, 0:2], in0=pp[:, 0:2], scalar1=0.5, scalar2=None,
                                                           op0=ALU.mult), reads=[ppb], writes=[prb])
                    tk.op("dve", lambda g: g.tensor_scalar(out=pr[:, UT + 2:UT + 4], in0=pp[:, 2:4], scalar1=0.5,
                                                           scalar2=None, op0=ALU.mult), reads=[ppb], writes=[prb])
            if "conv" in parts:
                for m in range(2):
                    ch = qk * 8 + h * 2 + m
                    ch_ = ch - ch_off
                    ps_ = (qk * 2 + m) % 2
                    pr, prb = B['pre'][m], B['pre_b'][m]
                    for tb in range(4):
                        a_ = tb % 2
                        ac, acb = B['acc'][a_], B['acc_b'][a_]
                        tk.op("act", lambda g: g.activation(out=ac[:], in_=pr[:, tb * 512:tb * 512 + 512],
                                                            func=AF.Identity, scale=cwt[:, ch_ * 5:ch_ * 5 + 1],
                                                            bias=cbh[:, ch:ch + 1]),
                              reads=[prb, cst, cbh_b] + cwdeps, writes=[acb])
                        for i in range(1, 5):
                            tk.op("dve", lambda g, i=i: g.scalar_tensor_tensor(
                                out=ac[:], in0=pr[:, tb * 512 + i:tb * 512 + i + 512],
                                scalar=cwt[:, ch_ * 5 + i:ch_ * 5 + i + 1], in1=ac[:], op0=ALU.mult, op1=ALU.add),
                                reads=[prb, cst, acb] + cwdeps, writes=[acb])
                        if silu:
                            tk.op("act", lambda g: g.activation(out=dstT[:, m, tb * 512:(tb + 1) * 512], in_=ac[:],
                                                                func=AF.Silu, scale=2.0),
                                  reads=[acb], writes=[dst_b[tb]])
                        else:
                            th_, th_b = tht[a_], tht_b[a_]
                            tk.op("act", lambda g: g.activation(out=th_[:], in_=ac[:], func=AF.Tanh),
                                  reads=[acb], writes=[th_b])
                            tk.op("dve", lambda g: g.scalar_tensor_tensor(out=dstT[:, m, tb * 512:(tb + 1) * 512],
                                                                          in0=th_[:], scalar=1.0, in1=ac[:],
                                                                          op0=ALU.add, op1=ALU.mult),
                                  reads=[acb, th_b], writes=[dst_b[tb]])
                        yield


        def v_proj(B, h, sc=None):
            wv, wvb = load_w(w_in, MV0 + h * 256)
            for t in range(16):
                e = t + 2
                pp, ppb = nextg()

                def f(g):
                    last = None
                    for fc in range(8):
                        last = g.matmul(pp[:, 0:256], lhsT=xnT[:, fc, e * 128:(e + 1) * 128], rhs=wv[:, fc, :],
                                        start=(fc == 0), stop=(fc == 7))
                    return last
                mm_group([ppb], [wvb, xnT_b[e]], f)
                if sc is None:
                    tk.op("act", lambda g: g.activation(out=B['vaug'][:, t, 0:256], in_=pp[:, 0:256], func=AF.Copy),
                          reads=[ppb], writes=[B['vaug_b'][t]])
                else:
                    tk.op("act", lambda g: g.activation(out=B['vaug'][:, t, 0:256], in_=pp[:, 0:256],
                                                        func=AF.Identity, scale=sc[0][:, t, h:h + 1]),
                          reads=[ppb, sc[1]], writes=[B['vaug_b'][t]])
                yield

        def k_trans(B, evac_eng="act"):
            for t in range(16):
                pb, pbb = nextb()

                def f(g):
                    g.transpose(out=pb[:, 0:128], in_=B['kT'][:, 0, t * 128:(t + 1) * 128], identity=ident[:])
                    return g.transpose(out=pb[:, 128:256], in_=B['kT'][:, 1, t * 128:(t + 1) * 128], identity=ident[:])
                mm_group([pbb], [B['kT_b'][t // 4], ident_b], f)
                if evac_eng == "act":
                    tk.op("act", lambda g: g.activation(out=B['ktok'][:, t, :], in_=pb[:, 0:256], func=AF.Copy),
                          reads=[pbb], writes=[B['ktok_b'][t]])
                else:
                    tk.op("dve", lambda g: g.tensor_copy(out=B['ktok'][:, t, :], in_=pb[:, 0:256]),
                          reads=[pbb], writes=[B['ktok_b'][t]])
                yield

        tinit_b = [tk.dma_buf("tinit%d" % i) for i in range(2)]
        scst_b = [[Buf("scst%d_%d" % (d_, h)) for h in range(4)] for d_ in range(2)]
        if prepass:
            ph = contextlib.ExitStack()
            sbp = phase_alloc(ph)
            PBs = []
            for k_ in range(2):
                PBs.append(dict(
                    pre=[sbp("pre%d_%d" % (k_, i), [128, UT + 4], F32) for i in range(2)],
                    pre_b=[Buf("pre%d_%d" % (k_, i)) for i in range(2)],
                    acc=[sbp("acc%d_%d" % (k_, i), [128, 512], F32) for i in range(2)],
                    acc_b=[Buf("acc%d_%d" % (k_, i)) for i in range(2)],
                    kT=sbp("kT%d" % k_, [128, 2, UT], BF16), kT_b=[Buf("kT%d_%d" % (k_, i)) for i in range(4)],
                    ktok=sbp("ktok%d" % k_, [128, 16, 256], BF16),
                    ktok_b=[Buf("ktok%d_%d" % (k_, i)) for i in range(16)],
                    vaug=sbp("vaug%d" % k_, [128, 16, 257], BF16),
                    vaug_b=[Buf("vaug%d_%d" % (k_, i)) for i in range(16)]))
                tk.op("dve", lambda g: g.memset(PBs[k_]["vaug"][:, :, 256:257], 1.0), writes=PBs[k_]["vaug_b"])
            evp2 = [sbp("evp%d" % k_, [128, 16, 4], F32) for k_ in range(2)]
            avp2 = [sbp("avp%d" % k_, [128, 16, 4], F32) for k_ in range(2)]
            gp_b2 = [[Buf("gp%d_%d" % (k_, t)) for t in range(16)] for k_ in range(2)]
            Tp = [sbp("Tp%d" % h, [128, 2, 257], F32) for h in range(4)]
            Tp_b = [Buf("Tp%d" % h) for h in range(4)]
            SF = [sbp("SF%d" % h, [128, 2, 257], F32) for h in range(4)]
            SF_b = [Buf("SF%d" % h) for h in range(4)]
            alast = sbp("alast", [128, 4], F32)
            alast_b = Buf("alast")
            flags = sbp("flags", [128, 16], F32)
            cwp = sbp("cwp", [128, 7, 40], F32)
            pgb = sbp("pgb", [128, 7, 8], F32)
            pvt = [sbp("pvt%d" % i, [128, 257], BF16) for i in range(3)]
            pvt_b = [Buf("pvt%d" % i) for i in range(3)]
            pst_b = tk.dma_buf("pst")
            tk.dma("sp", flags[:], c_pflags[:, :], writes=[pcst_b], track=pcst_b)
            tk.dma("sp", cwp[:].rearrange("p a b -> p (a b)"), c_pcw[:, :], writes=[pcst_b], track=pcst_b)
            tk.dma("sp", pgb[:].rearrange("p a b -> p (a b)"), c_pgb[:, :], writes=[pcst_b], track=pcst_b)
            for h in range(4):
                tk.op("dve", lambda g: g.memset(Tp[h][:], 0.0), writes=[Tp_b[h]])
                tk.op("dve", lambda g: g.memset(SF[h][:], 0.0), writes=[SF_b[h]])
            tk.op("dve", lambda g: g.memset(alast[:], 1.0), writes=[alast_b])

            xs1 = [sbp("xs1_%d" % k_, [128, 16, 4], F32) for k_ in range(2)]
            tots = [sbp("tots%d" % k_, [128, 16, 4], F32) for k_ in range(2)]
            Rv = [sbp("Rv%d" % k_, [128, 16, 4], F32) for k_ in range(2)]
            aslot2 = [sbp("aslot%d" % k_, [128, 8], F32) for k_ in range(2)]
            gs_b = [Buf("gs%d" % k_) for k_ in range(2)]

            def pre_slot_front(i):
                k_ = i % 2
                evp, gsb = evp2[k_], gs_b[k_]
                yield from stage0(lambda st_: xpre[i, st_ * 128:(st_ + 1) * 128, :],
                                  [(t, t + 2) for t in range(16)] + [(16, 20)])
                wg, wgb = load_w(c_pgw[i], 0, 8)
                for t in range(16):
                    e = t + 2
                    s = t % 2
                    pg, pgb_ = nextf()

                    def f(g):
                        last = None
                        for fc in range(8):
                            last = g.matmul(pg[:, 0:8], lhsT=xnT[:, fc, e * 128:(e + 1) * 128], rhs=wg[:, fc, 0:8],
                                            start=(fc == 0), stop=(fc == 7))
                        return last
                    mm_group([pgb_], [xnT_b[e], wgb], f)
                    g3 = gtmp[s][:, 0, :]
                    g2 = gtmp2[s]
                    tk.op("dve", lambda g: g.tensor_tensor(out=g3, in0=pg[:, 0:8], in1=pgb[:, i, :], op=ALU.add),
                          reads=[pgb_, pcst_b], writes=[gtmp_b[s]])
                    tk.op("act", lambda g: g.activation(out=g2[:, 0, 0, :], in_=gtmp[s][:, 0, 4:8], func=AF.Exp,
                                                        scale=-1.0), reads=[gtmp_b[s]], writes=[gtmp_b[s]])
                    tk.op("act", lambda g: g.activation(out=g2[:, 1, 0, :], in_=g2[:, 0, 0, :], func=AF.Ln,
                                                        bias=col[:, 0:1], scale=1.0),
                          reads=[gtmp_b[s], col_b], writes=[gtmp_b[s]])
                    pc, pcb = nextf()

                    def f(g):
                        g.matmul(pc[:, 0:4], lhsT=mask_f, rhs=g2[:, 1, 0, :], start=True, stop=True)
                        return g.matmul(pc[:, 4:8], lhsT=ones32, rhs=g2[:, 1, 0, :], start=True, stop=True)
                    mm_group([pcb], [gtmp_b[s], cst], f)
                    tk.op("dve", lambda g: g.tensor_tensor(out=g2[:, 2, 0, :], in0=pc[:, 0:4],
                                                           in1=gtmp[s][:, 0, 0:4], op=ALU.add),
                          reads=[pcb, gtmp_b[s]], writes=[gtmp_b[s]])
                    tk.op("dve", lambda g: g.tensor_tensor(out=xs1[k_][:, t, :], in0=g2[:, 2, 0, :], in1=pc[:, 4:8],
                                                           op=ALU.subtract),
                          reads=[pcb, gtmp_b[s]], writes=[gsb])
                    tk.op("act", lambda g: g.activation(out=tots[k_][:, t, :], in_=pc[:, 4:8], func=AF.Copy),
                          reads=[pcb], writes=[gsb])
                    yield
                tk.op("dve", lambda g: g.memset(Rv[k_][:, 15, :], 0.0), writes=[gsb])
                for t in range(14, -1, -1):
                    tk.op("dve", lambda g, t=t: g.tensor_tensor(out=Rv[k_][:, t, :], in0=Rv[k_][:, t + 1, :],
                                                                in1=tots[k_][:, t + 1, :], op=ALU.add),
                          reads=[gsb], writes=[gsb])
                tk.op("dve", lambda g: g.tensor_tensor(out=xs1[k_][:].rearrange("p a b -> p (a b)"),
                                                       in0=xs1[k_][:].rearrange("p a b -> p (a b)"),
                                                       in1=Rv[k_][:].rearrange("p a b -> p (a b)"), op=ALU.subtract),
                      reads=[gsb], writes=[gsb])
                tk.op("act", lambda g: g.activation(out=evp[:].rearrange("p a b -> p (a b)"),
                                                    in_=xs1[k_][:].rearrange("p a b -> p (a b)"), func=AF.Exp,
                                                    bias=col[:, 2:3], scale=1.0),
                      reads=[gsb, col_b], writes=[gsb])
                tk.op("dve", lambda g: g.tensor_tensor(out=aslot2[k_][:, 4:8], in0=Rv[k_][:, 0, :],
                                                       in1=tots[k_][:, 0, :], op=ALU.add), reads=[gsb], writes=[gsb])
                tk.op("act", lambda g: g.activation(out=aslot2[k_][:, 0:4], in_=aslot2[k_][:, 4:8], func=AF.Exp,
                                                    scale=-1.0), reads=[gsb], writes=[gsb])

            pitems = [(i, h) for i in range(7) for h in range(4)]

            def pre_front(n):
                i, h = pitems[n]
                if h == 0:
                    yield from pre_slot_front(i)
                yield from qk_proj(PBs[n % 2], h, 1, cwp[:, i, :], [pcst_b], 8, parts=("mm",))
                yield from v_proj(PBs[n % 2], h, sc=(evp2[i % 2], gs_b[i % 2]))
                tk.op("dve", lambda g: g.tensor_copy(out=PBs[n % 2]["vaug"][:, :, 256], in_=evp2[i % 2][:, :, h]),
                      reads=[gs_b[i % 2]], writes=PBs[n % 2]["vaug_b"])
                yield from qk_proj(PBs[n % 2], h, 1, cwp[:, i, :], [pcst_b], 8, parts=("conv",))

            def pre_chain(n, bgen):
                i, h = pitems[n]
                PB = PBs[n % 2]
                k_ = i % 2
                evp, gsb = evp2[k_], gs_b[k_]
                run(k_trans(PB, "dve"))
                pd0, pd0b = psf[4], psf_b[4]
                pd1, pd1b = psf[5], psf_b[5]
                for c in range(16):
                    adv(bgen, 5)
                    v_, v_b = PB["vaug"][:, c, :], PB["vaug_b"][c]

                    def f(g):
                        g.matmul(pd0[:, 0:257], lhsT=PB["ktok"][:, c, 0:128], rhs=v_, start=(c == 0),
                                 stop=(c == 15))
                        return g.matmul(pd1[:, 0:257], lhsT=PB["ktok"][:, c, 128:256], rhs=v_,
                                        start=(c == 0), stop=(c == 15))
                    mm_group([pd0b, pd1b], [PB["ktok_b"][c], v_b], f)
                c1, c1b = nextsm()
                tk.op("dve", lambda g: g.tensor_tensor(out=c1[:, 0:1], in0=aslot2[k_][:, h:h + 1],
                                                       in1=flags[:, i:i + 1], op=ALU.mult),
                      reads=[gsb, pcst_b], writes=[c1b])
                for m, (pd, pdb) in enumerate(((pd0, pd0b), (pd1, pd1b))):
                    tk.op("dve", lambda g, m=m, pd=pd: g.scalar_tensor_tensor(
                        out=Tp[h][:, m, :], in0=Tp[h][:, m, :], scalar=c1[:, 0:1], in1=pd[:, 0:257],
                        op0=ALU.mult, op1=ALU.add), reads=[pdb, Tp_b[h], c1b], writes=[Tp_b[h]])
                tk.op("dve", lambda g: g.scalar_tensor_tensor(
                    out=SF[h][:].rearrange("p a b -> p (a b)"), in0=Tp[h][:].rearrange("p a b -> p (a b)"),
                    scalar=flags[:, 7 + i:8 + i], in1=SF[h][:].rearrange("p a b -> p (a b)"),
                    op0=ALU.mult, op1=ALU.add),
                    reads=[Tp_b[h], SF_b[h], pcst_b], writes=[SF_b[h]])

            rr["n"] = 4
            run(pre_front(0))
            for n in range(len(pitems)):
                bgen = pre_front(n + 1) if n + 1 < len(pitems) else None
                pre_chain(n, bgen)
                if bgen is not None:
                    run(bgen)
            for h in range(4):
                c1, c1b = nextsm()
                tk.op("dve", lambda g: g.tensor_scalar(out=Tp[h][:].rearrange("p a b -> p (a b)"),
                                                       in0=Tp[h][:].rearrange("p a b -> p (a b)"),
                                                       scalar1=flags[:, 14:15], scalar2=None, op0=ALU.mult),
                      reads=[Tp_b[h], pcst_b], writes=[Tp_b[h]])
                tk.dma("sp", sc_st[0, h, :, :], SF[h][:].rearrange("p a b -> p (a b)"), reads=[SF_b[h]],
                       writes=[scst_b[0][h]], track=pst_b)
                tk.dma("sp", sc_st[1, h, :, :], Tp[h][:].rearrange("p a b -> p (a b)"), reads=[Tp_b[h]],
                       writes=[scst_b[1][h]], track=pst_b)
            for d_ in range(2):
                for h in range(4):
                    scst_b[d_][h].w = pst_b.last_dma
            barrier([pst_b, pcst_b])
            ph.close()
            rr["n"] = NPF

        for u in range(nunits):
            utype = 1 if u == 4 else 0
            run(stage0(lambda st_: xu[u, st_ * 128:(st_ + 1) * 128, :], [(e, e) for e in range(XT)]))
            if debug and u == 0:
                dt_ = sb("dbgt", [128, 1024], F32)
                dtb = tk.dma_buf("dbgt")
                dbg_b.append(dtb)
                tk.op("dve", lambda g: g.tensor_copy(out=dt_[:].rearrange("p (f t) -> p f t", f=8),
                                                     in_=xnT[:, :, 256:384]), reads=xnT_b, writes=[dtb])
                tk.dma("sp", dbg["xnT"][:, :], dt_[:], reads=[dtb], track=dtb)

            def xcols(e0, n):
                return slice(e0 * 128, e0 * 128 + n)

            wg, wgb = load_w(w_in, MG0, 16)
            for t in range(16):
                e = t + 2
                pg, pgb = nextf()

                def f(g):
                    last = None
                    for fc in range(8):
                        last = g.matmul(pg[:, 0:16], lhsT=xnT[:, fc, e * 128:(e + 1) * 128], rhs=wg[:, fc, 0:16],
                                        start=(fc == 0), stop=(fc == 7))
                    return last
                mm_group([pgb], [xnT_b[e], wgb], f)
                s = t % 2
                g3 = gtmp[s]
                g2 = gtmp2[s]
                tk.op("dve", lambda g: g.tensor_tensor(out=g3[:].rearrange("p a b -> p (a b)"), in0=pg[:, 0:16],
                                                       in1=bg[:, :], op=ALU.add),
                      reads=[pgb, cst], writes=[gtmp_b[s]])
                tk.op("act", lambda g: g.activation(out=g2[:, 0, :, :], in_=g3[:, :, 4:8], func=AF.Exp, scale=-1.0),
                      reads=[gtmp_b[s]], writes=[gtmp_b[s]])
                tk.op("act", lambda g: g.activation(out=g2[:, 1, :, :], in_=g2[:, 0, :, :], func=AF.Ln,
                                                    bias=col[:, 0:1], scale=1.0),
                      reads=[gtmp_b[s], col_b], writes=[gtmp_b[s]])
                pc, pcb = nextf()

                def f(g):
                    g.matmul(pc[:, 0:4], lhsT=mask_f, rhs=g2[:, 1, 0, :], start=True, stop=True)
                    g.matmul(pc[:, 4:8], lhsT=mask_b, rhs=g2[:, 1, 1, :], start=True, stop=True)
                    return g.matmul(pc[:, 8:16], lhsT=ones32, rhs=g2[:, 1, :, :].rearrange("p a b -> p (a b)"),
                                    start=True, stop=True)
                mm_group([pcb], [gtmp_b[s], cst], f)
                tk.op("dve", lambda g: g.tensor_tensor(out=g2[:, 2, :, :],
                                                       in0=pc[:, 0:8].rearrange("p (a b) -> p a b", a=2),
                                                       in1=g3[:, :, 0:4], op=ALU.add),
                      reads=[pcb, gtmp_b[s]], writes=[gtmp_b[s]])
                tk.op("act", lambda g: g.activation(out=ev[:, t, :, :], in_=g2[:, 2, :, :], func=AF.Exp,
                                                    bias=col[:, 2:3], scale=1.0),
                      reads=[gtmp_b[s], col_b], writes=[gate_b[t]])
                tk.op("act", lambda g: g.activation(out=rsv[:, t, :, :],
                                                    in_=pc[:, 0:8].rearrange("p (a b) -> p a b", a=2),
                                                    func=AF.Exp, scale=-1.0), reads=[pcb], writes=[gate_b[t]])
                tk.op("act", lambda g: g.activation(out=av[:, t, :, :],
                                                    in_=pc[:, 8:16].rearrange("p (a b) -> p a b", a=2),
                                                    func=AF.Exp, scale=-1.0), reads=[pcb], writes=[gate_b[t]])
                tk.op("dve", lambda g: g.tensor_scalar(out=nrsv[:, t, :, :], in0=rsv[:, t, :, :], scalar1=-1.0,
                                                       scalar2=None, op0=ALU.mult),
                      reads=[gate_b[t]], writes=[gate_b[t]])
            if debug and u == 0:
                dg = sb("dbgg", [128, 3 * 128], F32)
                dgb = tk.dma_buf("dbgg")
                dbg_b.append(dgb)
                for i, src in enumerate((ev, rsv, av)):
                    tk.op("dve", lambda g, i=i, src=src: g.tensor_copy(
                        out=dg[:, i * 128:(i + 1) * 128], in_=src[:].rearrange("p a b c -> p (a b c)")),
                        reads=gate_b, writes=[dgb])
                tk.dma("sp", dbg["gates"][:, :], dg[:], reads=[dgb], track=dgb)

            ph = contextlib.ExitStack()
            sbp = phase_alloc(ph)
            mhw = sbp("mhw", [128, D], F32)
            tk.dma("sp", mhw[:], c_mhw[:, :], writes=[pcst_b], track=pcst_b)
            acc = [sbp("acc%d" % i, [128, 512], F32) for i in range(2)]
            acc_b = [Buf("acc%d" % i) for i in range(2)]
            MBs = []
            _mpre = [sbp("pre%d" % i, [128, UT + 4], F32) for i in range(2)]
            _mpre_b = [Buf("pre%d" % i) for i in range(2)]
            for k_ in range(2):
                MBs.append(dict(
                    pre=_mpre, pre_b=_mpre_b, acc=acc, acc_b=acc_b,
                    qT=sbp("qT%d" % k_, [128, 2, UT], BF16), qT_b=[Buf("qT%d_%d" % (k_, i)) for i in range(4)],
                    kT=sbp("kT%d" % k_, [128, 2, UT], BF16), kT_b=[Buf("kT%d_%d" % (k_, i)) for i in range(4)],
                    ktok=sbp("ktok%d" % k_, [128, 16, 256], BF16),
                    ktok_b=[Buf("ktok%d_%d" % (k_, i)) for i in range(16)],
                    vaug=sbp("vaug%d" % k_, [128, 16, 257], BF16),
                    vaug_b=[Buf("vaug%d_%d" % (k_, i)) for i in range(16)]))
                tk.op("dve", lambda g: g.memset(MBs[k_]["vaug"][:, :, 256:257], 1.0), writes=MBs[k_]["vaug_b"])
            hpart = sbp("hpart", [128, 16, 256], BF16)
            hpart_b = [Buf("hpart%d" % i) for i in range(16)]
            Tst = [sbp("Tst%d" % i, [128, 2, 257], F32) for i in range(2)]
            Tst_b = [Buf("Tst%d" % i) for i in range(2)]
            Sbf = [sbp("Sbf%d" % i, [128, 2, 257], BF16) for i in range(2)]
            Sbf_b = [Buf("Sbf%d" % i) for i in range(2)]
            vt = [sbp("vt%d" % i, [128, 257], BF16) for i in range(4)]
            vt_b = [Buf("vt%d" % i) for i in range(4)]
            PT = [sbp("PT%d" % i, [128, 128], BF16) for i in range(4)]
            PT_b = [Buf("PT%d" % i) for i in range(4)]
            _fin = [sbp("fin%d" % j, [128, 512 if j == 1 else 256], F32) for j in range(4)]
            _finb = [Buf("fin%d" % j) for j in range(4)]
            fin, fin_b = [_fin, _fin], [_finb, _finb]
            yab = [sbp("yab%d" % i, [128, 256], BF16) for i in range(2)]
            yab_b = [Buf("yab%d" % i) for i in range(2)]

            def prep_gen(h_):
                B_ = MBs[h_ % 2]
                yield from qk_proj(B_, h_, 1, cw, [], 0, parts=("mm",))
                yield from v_proj(B_, h_)
                yield from qk_proj(B_, h_, 1, cw, [], 0, parts=("conv",), silu=True)
                yield from qk_proj(B_, h_, 0, cw, [], 0, silu=True)
                yield from k_trans(B_)

            run(prep_gen(0))
            rr["split"], rr["n"] = True, 4
            for h in range(4):
                MB = MBs[h % 2]
                qT, kT, ktok, vaug = MB["qT"], MB["kT"], MB["ktok"], MB["vaug"]
                qT_b, kT_b, ktok_b, vaug_b = MB["qT_b"], MB["kT_b"], MB["ktok_b"], MB["vaug_b"]
                bgen = prep_gen(h + 1) if h < 3 else None
                wo, wob = load_w(w_in, MO0 + h * 256)
                wz, wzb = load_w(w_in, MZ0 + h * 256)
                for d_ in range(2):
                    if prepass and u == 4:
                        tk.dma("sp", Tst[d_][:].rearrange("p a b -> p (a b)"), sc_st[d_, h, :, :],
                               reads=[scst_b[d_][h]], writes=[Tst_b[d_]], track=tinit_b[d_])
                        tk.op("act", lambda g: g.activation(out=Sbf[d_][:], in_=Tst[d_][:], func=AF.Copy),
                              reads=[Tst_b[d_]], writes=[Sbf_b[d_]])
                    else:
                        tk.op("dve", lambda g: g.memset(Tst[d_][:], 0.0), writes=[Tst_b[d_]])
                        tk.op("dve", lambda g: g.memset(Sbf[d_][:], 0.0), writes=[Sbf_b[d_]])
                vi = 0
                for step in range(16):
                    adv(bgen, 5)
                    for d_ in range(2):
                        c = step if d_ == 0 else 15 - step
                        cprev = c - 1 if d_ == 0 else c + 1
                        cs = slice(c * 128, (c + 1) * 128)
                        v_, v_b = vt[vi % 4], vt_b[vi % 4]
                        P_, P_b = PT[vi % 4], PT_b[vi % 4]
                        vi += 1
                        tk.prio = 0
                        tk.op("act", lambda g: g.activation(out=v_[:], in_=vaug[:, c, :], func=AF.Identity,
                                                            scale=ev[:, c, d_, h:h + 1]),
                              reads=[vaug_b[c], gate_b[c]], writes=[v_b])
                        pn, pnb = nextf()

                        def f(g):
                            g.matmul(pn[:, 257:385], lhsT=kT[:, 0, cs], rhs=qT[:, 0, cs], start=True, stop=False)
                            return g.matmul(pn[:, 257:385], lhsT=kT[:, 1, cs], rhs=qT[:, 1, cs], start=False,
                                            stop=True)
                        mm_group([pnb], [kT_b[c // 4], qT_b[c // 4]], f)
                        mk = mask_f if d_ == 0 else mask_b
                        tk.op("dve", lambda g: g.tensor_tensor(out=P_[:], in0=pn[:, 257:385], in1=mk, op=ALU.mult),
                              reads=[pnb, cst], writes=[P_b])

                        def f(g):
                            g.matmul(pn[:, 0:257], lhsT=qT[:, 0, cs], rhs=Sbf[d_][:, 0, :], start=True, stop=False)
                            g.matmul(pn[:, 0:257], lhsT=qT[:, 1, cs], rhs=Sbf[d_][:, 1, :], start=False, stop=False)
                            return g.matmul(pn[:, 0:257], lhsT=P_[:], rhs=v_[:], start=False, stop=True)
                        mm_group([pnb], [qT_b[c // 4], Sbf_b[d_], P_b, v_b], f)
                        pd0, pd0b = nextf()
                        pd1, pd1b = nextf()

                        def f(g):
                            g.matmul(pd0[:, 0:257], lhsT=ktok[:, c, 0:128], rhs=v_[:], start=True, stop=True)
                            return g.matmul(pd1[:, 0:257], lhsT=ktok[:, c, 128:256], rhs=v_[:], start=True, stop=True)
                        mm_group([pd0b, pd1b], [ktok_b[c], v_b], f)
                        if step == 0:
                            apv, apb = col[:, 0:1], [col_b]
                        else:
                            apv, apb = av[:, cprev, d_, h:h + 1], [gate_b[cprev]]
                        for m, (pd, pdb) in enumerate(((pd0, pd0b), (pd1, pd1b))):
                            tk.op("dve", lambda g, m=m, pd=pd: g.scalar_tensor_tensor(
                                out=Tst[d_][:, m, :], in0=Tst[d_][:, m, :], scalar=apv, in1=pd[:, 0:257],
                                op0=ALU.mult, op1=ALU.add), reads=[pdb, Tst_b[d_]] + apb, writes=[Tst_b[d_]])
                        tk.op("act", lambda g: g.activation(out=Sbf[d_][:], in_=Tst[d_][:], func=AF.Identity,
                                                            scale=av[:, c, d_, h:h + 1]),
                              reads=[Tst_b[d_], gate_b[c]], writes=[Sbf_b[d_]])
                        tk.prio = 1
                        c1, c1b = nextsm()
                        rs_ = rsv[:, c, d_, h:h + 1]
                        tk.op("dve", lambda g: g.tensor_scalar(out=c1[:, 3:4], in0=pn[:, 256:257], scalar1=rs_,
                                                               scalar2=1.0, op0=ALU.mult, op1=ALU.max),
                              reads=[pnb, gate_b[c]], writes=[c1b])
                        tk.op("dve", lambda g: g.scalar_tensor_tensor(out=c1[:, 0:1], in0=pn[:, 256:257],
                                                                      scalar=nrsv[:, c, d_, h:h + 1],
                                                                      in1=c1[:, 3:4], op0=ALU.mult, op1=ALU.max),
                              reads=[pnb, gate_b[c], c1b], writes=[c1b])
                        tk.op("dve", lambda g: g.reciprocal(out=c1[:, 1:2], in_=c1[:, 0:1]), reads=[c1b], writes=[c1b])
                        tk.op("dve", lambda g: g.scalar_tensor_tensor(out=c1[:, 2:3], in0=c1[:, 1:2], scalar=0.5,
                                                                      in1=rs_, op0=ALU.mult, op1=ALU.mult),
                              reads=[c1b, gate_b[c]], writes=[c1b])
                        if step < 8:
                            tk.op("act", lambda g: g.activation(out=hpart[:, c, :], in_=pn[:, 0:256], func=AF.Identity,
                                                                scale=c1[:, 2:3]), reads=[pnb, c1b],
                                  writes=[hpart_b[c]])
                            continue
                        fs = c % 2
                        f0, f1, f2, f3 = fin[fs]
                        fb0, fb1, fb2, fb3 = fin_b[fs]
                        tk.op("dve", lambda g: g.scalar_tensor_tensor(out=f0[:], in0=pn[:, 0:256], scalar=c1[:, 2:3],
                                                                      in1=hpart[:, c, :], op0=ALU.mult, op1=ALU.add),
                              reads=[pnb, c1b, hpart_b[c]], writes=[fb0])
                        po, pob = nextf()
                        e = c + 2

                        def f(g):
                            last = None
                            for fc in range(8):
                                g.matmul(po[:, 0:256], lhsT=xnT[:, fc, e * 128:(e + 1) * 128], rhs=wo[:, fc, :],
                                         start=(fc == 0), stop=(fc == 7))
                            for fc in range(8):
                                last = g.matmul(po[:, 256:512], lhsT=xnT[:, fc, e * 128:(e + 1) * 128],
                                                rhs=wz[:, fc, :], start=(fc == 0), stop=(fc == 7))
                            return last
                        mm_group([pob], [xnT_b[e], wob, wzb], f)
                        tk.op("act", lambda g: g.activation(out=f1[:, 0:256], in_=po[:, 0:256], func=AF.Tanh,
                                                            scale=0.5), reads=[pob], writes=[fb1])
                        tk.op("act", lambda g: g.activation(out=f2[:], in_=po[:, 256:512], func=AF.Silu),
                              reads=[pob], writes=[fb2])
                        tk.op("dve", lambda g: g.scalar_tensor_tensor(out=f0[:], in0=f1[:, 0:256], scalar=1.0,
                                                                      in1=f0[:], op0=ALU.add, op1=ALU.mult),
                              reads=[fb0, fb1], writes=[fb0])
                        c2, c2b = nextsm()
                        tk.op("act", lambda g: g.activation(out=f3[:], in_=f0[:], func=AF.Square,
                                                            accum_out=c2[:, 0:1]),
                              reads=[fb0], writes=[fb3, c2b])
                        tk.op("dve", lambda g: g.tensor_scalar(out=c2[:, 1:2], in0=c2[:, 0:1], scalar1=1.0 / 256,
                                                               scalar2=EPS, op0=ALU.mult, op1=ALU.add),
                              reads=[c2b], writes=[c2b])
                        tk.op("pool", lambda g: g.tensor_tensor(out=c2[:, 2:3], in0=c2[:, 1:2], in1=col[:, 1:2],
                                                                op=ALU.pow), reads=[c2b, col_b], writes=[c2b])
                        tk.op("dve", lambda g: g.scalar_tensor_tensor(out=f3[:], in0=f0[:], scalar=c2[:, 2:3],
                                                                      in1=mhw[:, h * 256:(h + 1) * 256],
                                                                      op0=ALU.mult, op1=ALU.mult),
                              reads=[fb0, c2b, pcst_b], writes=[fb3])
                        ya_, ya_b = yab[fs], yab_b[fs]
                        tk.op("dve", lambda g: g.tensor_tensor(out=ya_[:], in0=f3[:], in1=f2[:], op=ALU.mult),
                              reads=[fb3, fb2], writes=[ya_b])
                        pb, pbb = nextb()

                        def f(g):
                            g.transpose(out=pb[:, 0:128], in_=ya_[:, 0:128], identity=ident[:])
                            return g.transpose(out=pb[:, 128:256], in_=ya_[:, 128:256], identity=ident[:])
                        mm_group([pbb], [ya_b, ident_b], f)
                        yi = ystr["i"] % 4
                        ystr["i"] += 1
                        tk.op("act", lambda g: g.activation(out=yst[yi][:].rearrange("p a b -> p (a b)"),
                                                            in_=pb[:, 0:256], func=AF.Copy),
                              reads=[pbb], writes=[yst_b[yi]])
                        tk.dma("sp", sc_ya[u, :, 2 * h:2 * h + 2, c * 128:(c + 1) * 128], yst[yi][:],
                               reads=[yst_b[yi]], writes=[scya_b[u][c]], track=yst_b[yi])
                if bgen is not None:
                    run(bgen)

            rr["split"], rr["n"] = False, NPF
            barrier()
            ph.close()
            ph = contextlib.ExitStack()
            sbp = phase_alloc(ph)
            eb_res = sbp("eb_res", [128, 16 * 5 * 128], BF16)
            eb_res_b = Buf("eb_res")
            nstage = [sbp("nstage%d" % i, [128, 1280], F32) for i in range(2)]
            for i in range(8):
                s = i % 2
                tk.dma("sp", nstage[s][:], c_nai[:, i * 1280:(i + 1) * 1280], writes=[nstage_b[s]], track=nstage_b[s])
                tk.op("act", lambda g: g.activation(out=eb_res[:, i * 1280:(i + 1) * 1280], in_=nstage[s][:],
                                                    func=AF.Exp), reads=[nstage_b[s]], writes=[eb_res_b])
            NBs = []
            for k_ in range(2):
                NBs.append(dict(
                    nqT=sbp("nqT%d" % k_, [128, UT], BF16), nqT_b=[Buf("nqT%d_%d" % (k_, i)) for i in range(4)],
                    nkT=sbp("nkT%d" % k_, [128, 2560], BF16), nkT_b=[Buf("nkT%d_%d" % (k_, i)) for i in range(5)],
                    nv=sbp("nv%d" % k_, [128, 20, 2, 65], BF16), nv_b=[Buf("nv%d_%d" % (k_, i)) for i in range(20)],
                    nz=sbp("nz%d" % k_, [128, 16, 128], F32), nz_b=[Buf("nz%d_%d" % (k_, i)) for i in range(16)]))
                tk.op("dve", lambda g: g.memset(NBs[k_]["nv"][:, :, :, 64:65], 2.0), writes=NBs[k_]["nv_b"])
            eb_s = [sbp("eb_s%d" % i, [128, 2, 5, 128], BF16) for i in range(2)]
            eb_s_b = [Buf("eb_s%d" % i) for i in range(2)]
            ex = [sbp("ex%d" % i, [128, 640], BF16) for i in range(2)]
            ex_b = [Buf("ex%d" % i) for i in range(2)]
            et = [sbp("et%d" % i, [128, 640], BF16) for i in range(3)]
            et_b = [Buf("et%d" % i) for i in range(3)]
            ybb = [sbp("ybb%d" % i, [128, 128], BF16) for i in range(2)]
            ybb_b = [Buf("ybb%d" % i) for i in range(2)]
            def na_proj(NB, hp):
                nqT, nkT, nv, nz = NB['nqT'], NB['nkT'], NB['nv'], NB['nz']
                nqT_b, nkT_b, nv_b, nz_b = NB['nqT_b'], NB['nkT_b'], NB['nv_b'], NB['nz_b']
                wq, wqb = load_w(w_in, NQ0 + hp * 128, 128)
                wk, wkb = load_w(w_in, NK0 + hp * 128, 128)
                wv, wvb = load_w(w_in, NV0 + hp * 128, 128)
                wz, wzb = load_w(w_in, NZ0 + hp * 128, 128)
                for tb in range(4):
                    pp, ppb = nextg()

                    def f(g):
                        last = None
                        for fc in range(8):
                            last = g.matmul(pp[:, 0:512], lhsT=wq[:, fc, 0:128],
                                            rhs=xnT[:, fc, 256 + tb * 512:256 + (tb + 1) * 512],
                                            start=(fc == 0), stop=(fc == 7))
                        return last
                    mm_group([ppb], [wqb] + xnT_b[2 + tb * 4:2 + tb * 4 + 4], f)
                    tk.op("act", lambda g: g.activation(out=nqT[:, tb * 512:(tb + 1) * 512], in_=pp[:, 0:512],
                                                        func=AF.Identity, scale=0.125), reads=[ppb], writes=[nqT_b[tb]])
                    yield
                for tb in (range(5) if utype == 1 else ()):
                    pp, ppb = nextg()

                    def f(g):
                        last = None
                        for fc in range(8):
                            last = g.matmul(pp[:, 0:512], lhsT=wk[:, fc, 0:128],
                                            rhs=xnT[:, fc, tb * 512:(tb + 1) * 512],
                                            start=(fc == 0), stop=(fc == 7))
                        return last
                    mm_group([ppb], [wkb] + xnT_b[tb * 4:tb * 4 + 4], f)
                    tk.op("dve", lambda g: g.tensor_copy(out=nkT[:, tb * 512:(tb + 1) * 512], in_=pp[:, 0:512]),
                          reads=[ppb], writes=[nkT_b[tb]])
                    yield
                if utype == 0:
                    for tb in range(4):
                        pp, ppb = nextg()

                        def f(g):
                            last = None
                            for fc in range(8):
                                last = g.matmul(pp[:, 0:512], lhsT=wk[:, fc, 0:128],
                                                rhs=xnT[:, fc, 256 + tb * 512:256 + (tb + 1) * 512],
                                                start=(fc == 0), stop=(fc == 7))
                            return last
                        mm_group([ppb], [wkb] + xnT_b[2 + tb * 4:2 + tb * 4 + 4], f)
                        tk.op("dve", lambda g: g.tensor_copy(out=nkT[:, 256 + tb * 512:256 + (tb + 1) * 512],
                                                             in_=pp[:, 0:512]), reads=[ppb], writes=nkT_b)
                        yield
                    tk.op("act", lambda g: g.activation(out=nkT[:, 0:256], in_=nkT[:, 512:768], func=AF.Copy),
                          reads=nkT_b, writes=nkT_b)
                    tk.op("act", lambda g: g.activation(out=nkT[:, 2304:2560], in_=nkT[:, 1792:2048], func=AF.Copy),
                          reads=nkT_b, writes=nkT_b)
                for e in (range(20) if utype == 1 else range(2, 18)):
                    pp, ppb = nextg()

                    def f(g):
                        last = None
                        for fc in range(8):
                            last = g.matmul(pp[:, 0:128], lhsT=xnT[:, fc, e * 128:(e + 1) * 128], rhs=wv[:, fc, 0:128],
                                            start=(fc == 0), stop=(fc == 7))
                        return last
                    mm_group([ppb], [wvb, xnT_b[e]], f)
                    tk.op("act", lambda g: g.activation(out=nv[:, e, :, 0:64],
                                                        in_=pp[:, 0:128].rearrange("p (a b) -> p a b", a=2),
                                                        func=AF.Copy), reads=[ppb], writes=[nv_b[e]])
                    yield
                if utype == 0:
                    for dst_e, src_e in ((0, 4), (1, 5), (18, 14), (19, 15)):
                        tk.op("dve", lambda g, dst_e=dst_e, src_e=src_e: g.tensor_copy(
                            out=nv[:, dst_e, :, 0:64], in_=nv[:, src_e, :, 0:64]),
                            reads=[nv_b[src_e]], writes=[nv_b[dst_e]])
                for t in range(16):
                    e = t + 2
                    pp, ppb = nextg()

                    def f(g):
                        last = None
                        for fc in range(8):
                            last = g.matmul(pp[:, 0:128], lhsT=xnT[:, fc, e * 128:(e + 1) * 128], rhs=wz[:, fc, 0:128],
                                            start=(fc == 0), stop=(fc == 7))
                        return last
                    mm_group([ppb], [wzb, xnT_b[e]], f)
                    th_, th_b = tht[t % 2], tht_b[t % 2]
                    tk.op("act", lambda g: g.activation(out=th_[:, 0:128], in_=pp[:, 0:128], func=AF.Tanh, scale=0.5),
                          reads=[ppb], writes=[th_b])
                    tk.op("dve", lambda g: g.scalar_tensor_tensor(out=nz[:, t, :], in0=th_[:, 0:128], scalar=1.0,
                                                                  in1=pp[:, 0:128], op0=ALU.add, op1=ALU.mult),
                          reads=[ppb, th_b], writes=[nz_b[t]])
                    yield

            run(na_proj(NBs[0], 0))
            rr["split"], rr["n"] = True, 4
            for hp in range(8):
                NB = NBs[hp % 2]
                nqT, nkT, nv, nz = NB['nqT'], NB['nkT'], NB['nv'], NB['nz']
                nqT_b, nkT_b, nv_b, nz_b = NB['nqT_b'], NB['nkT_b'], NB['nv_b'], NB['nz_b']
                bgen = na_proj(NBs[(hp + 1) % 2], hp + 1) if hp < 7 else None
                def na_scores(n):
                    j, hh = n // 2, n % 2
                    if j in JB:
                        jb = JB.index(j)
                        s = (hp * 4 + jb) % 2
                        if hh == 0:
                            idx = (utype * 4 + jb) * 8 + hp
                            tk.dma("sp", nstage[s][:], c_nab[idx, :, :], writes=[nstage_b[s]], track=nstage_b[s])
                            tk.op("act", lambda g: g.activation(out=eb_s[s][:].rearrange("p a b c -> p (a b c)"),
                                                                in_=nstage[s][:], func=AF.Exp),
                                  reads=[nstage_b[s]], writes=[eb_s_b[s]])
                        ebv, ebb = eb_s[s][:, hh, :, :].rearrange("p b c -> p (b c)"), eb_s_b[s]
                    else:
                        o0 = (2 * hp + hh) * 640
                        ebv, ebb = eb_res[:, o0:o0 + 640], eb_res_b
                    base = hh * 64
                    xi = n % 2
                    ti = n % 3
                    pa, pab = nextf()
                    pc2, pc2b = nextf()

                    def f(g):
                        last = None
                        for kt in range(5):
                            e = j + kt
                            dst = pa[:, kt * 128:(kt + 1) * 128] if kt < 4 else pc2[:, 0:128]
                            last = g.matmul(dst, lhsT=nkT[base:base + 64, e * 128:(e + 1) * 128],
                                            rhs=nqT[base:base + 64, j * 128:(j + 1) * 128], start=True, stop=True)
                        return last
                    mm_group([pab, pc2b], [nqT_b[j // 4]] + [nkT_b[(j + kt) // 4] for kt in range(5)], f)
                    tk.op("act", lambda g: g.activation(out=ex[xi][:, 0:512], in_=pa[:, 0:512], func=AF.Exp),
                          reads=[pab], writes=[ex_b[xi]])
                    tk.op("act", lambda g: g.activation(out=ex[xi][:, 512:640], in_=pc2[:, 0:128], func=AF.Exp),
                          reads=[pc2b], writes=[ex_b[xi]])
                    tk.op("dve", lambda g: g.tensor_tensor(out=et[ti][:], in0=ex[xi][:], in1=ebv, op=ALU.mult),
                          reads=[ex_b[xi], ebb], writes=[et_b[ti]])

                def na_pv(n):
                    j, hh = n // 2, n % 2
                    base = hh * 64
                    ti = n % 3
                    yb_, yb_b = ybb[j % 2], ybb_b[j % 2]
                    po, pob = nextf()

                    def f(g):
                        last = None
                        for kt in range(5):
                            last = g.matmul(po[:, 0:65], lhsT=et[ti][:, kt * 128:(kt + 1) * 128],
                                            rhs=nv[:, j + kt, hh, :], start=(kt == 0), stop=(kt == 4))
                        return last
                    mm_group([pob], [et_b[ti]] + [nv_b[j + kt] for kt in range(5)], f)
                    c1, c1b = nextsm()
                    tk.op("dve", lambda g: g.reciprocal(out=c1[:, 0:1], in_=po[:, 64:65]), reads=[pob],
                          writes=[c1b])
                    tk.op("dve", lambda g: g.scalar_tensor_tensor(
                        out=yb_[:, base:base + 64], in0=po[:, 0:64], scalar=c1[:, 0:1],
                        in1=nz[:, j, base:base + 64], op0=ALU.mult, op1=ALU.mult),
                        reads=[pob, c1b, nz_b[j]], writes=[yb_b])
                    if hh == 0:
                        return
                    pb, pbb = nextb()
                    mm_group([pbb], [yb_b, ident_b],
                             lambda g: g.transpose(out=pb[:, 0:128], in_=yb_[:], identity=ident[:]))
                    yi = ystr["i"] % 4
                    ystr["i"] += 1
                    tk.op("act", lambda g: g.activation(out=yst[yi][:, 0, :], in_=pb[:, 0:128], func=AF.Copy),
                          reads=[pbb], writes=[yst_b[yi]])
                    tk.dma("sp", sc_yb[u, :, hp, j * 128:(j + 1) * 128], yst[yi][:, 0, :],
                           reads=[yst_b[yi]], writes=[scyb_b[u][j]], track=yst_b[yi])

                na_scores(0)
                for n in range(32):
                    adv(bgen, 2)
                    if n + 1 < 32:
                        na_scores(n + 1)
                    na_pv(n)
                if bgen is not None:
                    run(bgen)

            rr["split"], rr["n"] = False, NPF
            barrier(nstage_b)
            ph.close()
            ph = contextlib.ExitStack()
            sbp = phase_alloc(ph)
            fnw = sbp("fnw", [128, D], F32)
            tk.dma("sp", fnw[:], c_fnw[:, :], writes=[pcst_b], track=pcst_b)
            ytl = [sbp("ytl%d" % i, [128, 8, 1024], BF16) for i in range(2)]
            mT = sbp("mT", [128, 8, 1024], BF16)
            mT_b = [Buf("mT%d" % i) for i in range(8)]
            tt_f = [sbp("ttf%d" % i, [128, 512], F32) for i in range(4)]
            tt_b = [Buf("ttf%d" % i) for i in range(4)]
            res = [sbp("res%d" % i, [128, D], F32) for i in range(2)]
            for tb in range(2):
                xdeps = xnT_b[2 + tb * 8:2 + tb * 8 + 8]
                tk.dma("sp", ytl[0][:], sc_ya[u, :, :, tb * 1024:(tb + 1) * 1024], reads=scya_b[u][tb * 8:tb * 8 + 8],
                       writes=[ytl_b[0]], track=ytl_b[0])
                tk.dma("sp", ytl[1][:], sc_yb[u, :, :, tb * 1024:(tb + 1) * 1024], reads=scyb_b[u][tb * 8:tb * 8 + 8],
                       writes=[ytl_b[1]], track=ytl_b[1])
                for f2_ in range(4):
                    wts = [load_w(w_da, f2_ * 256), load_w(w_in, GA0 + f2_ * 256),
                           load_w(w_db, f2_ * 256), load_w(w_in, GB0 + f2_ * 256)]
                    for m in range(2):
                        fo = f2_ * 2 + m
                        for hf in range(2):
                            tcols = slice(256 + tb * 1024 + hf * 512, 256 + tb * 1024 + (hf + 1) * 512)
                            outs = []
                            for br in range(2):
                                wd, wdb_ = wts[br * 2]
                                wg_, wgb_ = wts[br * 2 + 1]
                                pdn, pdnb = nextf()

                                def f(g):
                                    last = None
                                    for fc in range(8):
                                        last = g.matmul(pdn[:, 0:512], lhsT=wd[:, fc, m * 128:(m + 1) * 128],
                                                        rhs=ytl[br][:, fc, hf * 512:(hf + 1) * 512], start=(fc == 0),
                                                        stop=(fc == 7))
                                    return last
                                mm_group([pdnb], [wdb_, ytl_b[br]], f)
                                pgt, pgtb = nextf()

                                def f(g):
                                    last = None
                                    for fc in range(8):
                                        last = g.matmul(pgt[:, 0:512], lhsT=wg_[:, fc, m * 128:(m + 1) * 128],
                                                        rhs=xnT[:, fc, tcols], start=(fc == 0), stop=(fc == 7))
                                    return last
                                mm_group([pgtb], [wgb_] + xdeps, f)
                                sg, sgb = tt_f[br * 2], tt_b[br * 2]
                                pr_, prb_ = tt_f[br * 2 + 1], tt_b[br * 2 + 1]
                                tk.op("act", lambda g: g.activation(out=sg[:], in_=pgt[:, 0:512], func=AF.Tanh, scale=0.5),
                                      reads=[pgtb], writes=[sgb])
                                tk.op("dve", lambda g: g.scalar_tensor_tensor(out=pr_[:], in0=sg[:], scalar=1.0,
                                                                              in1=pdn[:, 0:512], op0=ALU.add, op1=ALU.mult),
                                      reads=[pdnb, sgb], writes=[prb_])
                                outs.append((pr_, prb_))
                            tk.op("dve", lambda g: g.tensor_tensor(out=mT[:, fo, hf * 512:(hf + 1) * 512],
                                                                   in0=outs[0][0][:], in1=outs[1][0][:],
                                                                   op=ALU.add),
                                  reads=[outs[0][1], outs[1][1]], writes=[mT_b[fo]])
                wos = [load_w(w_out, i * 256) for i in range(4)]
                for tl in range(8):
                    t = tb * 8 + tl
                    rs_i = t % 2
                    r_, r_b = res[rs_i], res_b[rs_i]
                    x_, x_b = xs[rs_i], xs_b[rs_i]
                    tk.dma("sp", x_[:], xu[u, 256 + t * 128:256 + (t + 1) * 128, :], writes=[x_b], track=x_b)
                    for half in range(2):
                        po, pob = nextf()

                        def f(g):
                            last = None
                            for q4 in range(2):
                                wo_, _ = wos[half * 2 + q4]
                                for fc in range(8):
                                    last = g.matmul(po[:, q4 * 256:(q4 + 1) * 256],
                                                    lhsT=mT[:, fc, tl * 128:(tl + 1) * 128], rhs=wo_[:, fc, :],
                                                    start=(fc == 0), stop=(fc == 7))
                            return last
                        mm_group([pob], mT_b + [wos[half * 2][1], wos[half * 2 + 1][1]], f)
                        tk.op("dve", lambda g: g.scalar_tensor_tensor(out=r_[:, half * 512:(half + 1) * 512],
                                                                      in0=po[:, 0:512], scalar=0.5,
                                                                      in1=x_[:, half * 512:(half + 1) * 512],
                                                                      op0=ALU.mult, op1=ALU.add),
                              reads=[pob, x_b], writes=[r_b])
                    c1, c1b = nextsm()
                    tk.op("act", lambda g: g.activation(out=junk[:], in_=r_[:], func=AF.Square, accum_out=c1[:, 0:1]),
                          reads=[r_b], writes=[junk_b, c1b])
                    tk.op("dve", lambda g: g.tensor_scalar(out=c1[:, 1:2], in0=c1[:, 0:1], scalar1=1.0 / D,
                                                           scalar2=EPS, op0=ALU.mult, op1=ALU.add),
                          reads=[c1b], writes=[c1b])
                    tk.op("pool", lambda g: g.tensor_tensor(out=c1[:, 2:3], in0=c1[:, 1:2], in1=col[:, 1:2],
                                                            op=ALU.pow), reads=[c1b, col_b], writes=[c1b])
                    tk.op("dve", lambda g: g.scalar_tensor_tensor(out=r_[:], in0=r_[:], scalar=c1[:, 2:3],
                                                                  in1=fnw[:, :], op0=ALU.mult, op1=ALU.mult),
                          reads=[r_b, c1b, pcst_b], writes=[r_b])
                    tk.dma("sp", yu[u, t * 128:(t + 1) * 128, :], r_[:], reads=[r_b], track=r_b)

            barrier(res_b + ytl_b + yst_b + [pcst_b])
            ph.close()

        tk.wait_all("sp", res_b + yst_b + dbg_b)
    return nc


def _na_tables(rpb, tc, bc, js):
    H = rpb.shape[0]
    out = np.full((len(js), H, 128, 5, 128), NEGB, np.float32)
    qc = np.arange(64)
    kc = np.arange(64)
    cs = np.clip(qc - 8, 0, 48)
    cvalid = (kc[:, None] >= cs[None, :]) & (kc[:, None] < cs[None, :] + 16)
    dci = np.clip(kc[:, None] - qc[None, :] + 15, 0, 30)
    for ji, j in enumerate(js):
        for kt in range(5):
            for a in range(2):
                s = 2 * (j + kt) - 4 + a
                if s < 0:
                    kr = 8 + s if tc else s
                    dup = tc and (kr <= 2 * j + 5)
                elif s >= 32:
                    if s == 35:
                        continue
                    kr = s - 8 if bc else s
                    dup = bc and (kr >= 2 * j - 4)
                else:
                    kr = s
                    dup = False
                if dup:
                    continue
                for b in range(2):
                    r = 2 * j + b
                    lo = r - 4
                    if tc:
                        lo = max(lo, 0)
                    if bc:
                        lo = min(lo, 24)
                    if not (lo <= kr <= lo + 7):
                        continue
                    dr = kr - r
                    blk = rpb[:, dr + 7, :][:, dci]
                    blk = np.where(cvalid[None], blk, np.float32(NEGB))
                    out[ji, :, a * 64:(a + 1) * 64, kt, b * 64:(b + 1) * 64] = blk
    return out


def _unit_x(seq, top, bot, conv4):
    blk = np.zeros((XROWS, D), np.float32)
    blk[0:256] = top
    blk[256:256 + UT] = seq
    blk[2304:2304 + 192] = bot
    blk[2560:2564] = conv4
    return blk


def _prep_inputs(inputs):
    f = lambda a: np.ascontiguousarray(np.asarray(a), dtype=np.float32)
    xp = f(inputs["x_prompt"])[0]
    xsm = f(inputs["x_sample"])
    w_in = f(inputs["w_in"])[0]
    rpb = f(inputs["rpb"])[0]
    conv_w = f(inputs["conv_w"])[0]
    conv_b = f(inputs["conv_b"])[0]
    bgate = f(inputs["b_gate"])[0]
    common = {
        "w_in": w_in,
        "w_da": f(inputs["w_down_a"])[0],
        "w_db": f(inputs["w_down_b"])[0],
        "w_out": f(inputs["w_out"])[0],
        "c_normT": np.ascontiguousarray(f(inputs["norm_w"])[0].reshape(8, 128).T),
        "c_normb": np.ascontiguousarray(np.broadcast_to(f(inputs["norm_w"])[0][None, :], (128, D))),
        "c_cw": np.ascontiguousarray(conv_w.reshape(5, 16, 128).transpose(2, 1, 0).reshape(128, 80)),
        "c_cb": np.ascontiguousarray(conv_b.reshape(16, 128).T),
        "c_bg": np.ascontiguousarray(np.broadcast_to(f(inputs["b_gate"])[0][None, :], (128, 16))),
        "c_mhw": np.ascontiguousarray(np.broadcast_to(f(inputs["mh_norm_w"])[0][None, :], (128, D))),
        "c_fnw": np.ascontiguousarray(np.broadcast_to(f(inputs["final_norm_w"])[None, :], (128, D))),
        "c_ident": np.eye(128, dtype=np.float32),
    }
    s_ = np.arange(128)
    mf = (s_[:, None] <= s_[None, :]).astype(np.float32)
    mb = (s_[:, None] >= s_[None, :]).astype(np.float32)
    common["c_mask"] = np.ascontiguousarray(np.concatenate([mf, mb, np.ones((128, 128), np.float32)], axis=1))
    nai = _na_tables(rpb, False, False, [5])[0]
    common["c_nai"] = np.ascontiguousarray(nai.transpose(1, 0, 2, 3).reshape(128, 16 * 5 * 128))

    def nab_for(tc, bc):
        t = _na_tables(rpb, tc, bc, list(JB))
        t = t.reshape(4, 8, 2, 128, 5, 128).transpose(0, 1, 3, 2, 4, 5)
        return t.reshape(4 * 8, 128, 1280)
    nab_sample = nab_for(True, True)
    z4 = np.zeros((4, D), np.float32)
    in_maps = []
    for c in range(NCORES):
        xu = np.zeros((NUNITS, XROWS, D), np.float32)
        for i in range(4):
            sq = xsm[4 * c + i]
            xu[i] = _unit_x(sq, sq[256:512], sq[24 * 64:27 * 64], z4)
        t0 = c * UT
        seg = xp[t0:t0 + UT]
        top = xp[256:512] if c == 0 else xp[t0 - 256:t0]
        bot = xp[248 * 64:251 * 64] if c == NCORES - 1 else xp[t0 + UT:t0 + UT + 192]
        cv = z4.copy()
        if c > 0:
            cv[0:2] = xp[t0 - 2:t0]
        if c < NCORES - 1:
            cv[2:4] = xp[t0 + UT:t0 + UT + 2]
        xu[4] = _unit_x(seg, top, bot, cv)
        nab = np.concatenate([nab_sample, nab_for(c == 0, c == NCORES - 1)], axis=0)
        m = dict(common)
        m["xu"] = xu
        m["c_nab"] = np.ascontiguousarray(nab)
        xpre = np.zeros((7, 17 * 128, D), np.float32)
        pgw = np.zeros((7, D, 8), np.float32)
        pgb = np.zeros((128, 7, 8), np.float32)
        pcw = np.zeros((128, 7, 8, 5), np.float32)
        flags = np.zeros((128, 16), np.float32)
        for i in range(7):
            if i < c:
                sg, flip = i, False
            else:
                sg, flip = 7 - (i - c), True
            s0 = sg * UT
            rows = xp[s0:s0 + UT]
            halo = np.zeros((4, D), np.float32)
            if not flip:
                if s0 >= 2:
                    halo[0:2] = xp[s0 - 2:s0]
                if s0 + UT + 2 <= 16384:
                    halo[2:4] = xp[s0 + UT:s0 + UT + 2]
                gi, gf = slice(MG0, MG0 + 4), slice(MG0 + 4, MG0 + 8)
                taps = [0, 1, 2, 3, 4]
            else:
                rows = rows[::-1]
                if s0 + UT + 2 <= 16384:
                    halo[0] = xp[s0 + UT + 1]
                    halo[1] = xp[s0 + UT]
                if s0 >= 2:
                    halo[2] = xp[s0 - 1]
                    halo[3] = xp[s0 - 2]
                gi, gf = slice(MG0 + 8, MG0 + 12), slice(MG0 + 12, MG0 + 16)
                taps = [4, 3, 2, 1, 0]
            xpre[i, 0:UT] = rows
            xpre[i, UT:UT + 4] = halo
            pgw[i, :, 0:4] = w_in[:, gi]
            pgw[i, :, 4:8] = w_in[:, gf]
            pgb[:, i, 0:4] = bgate[gi.start - MG0:gi.stop - MG0][None, :]
            pgb[:, i, 4:8] = bgate[gf.start - MG0:gf.stop - MG0][None, :]
            pcw[:, i] = conv_w[taps][:, 1024:2048].reshape(5, 8, 128).transpose(2, 1, 0)
            flags[:, i] = 0.0 if i == c else 1.0
            flags[:, 7 + i] = 1.0 if i == c - 1 else 0.0
        flags[:, 14] = 1.0 if c < NCORES - 1 else 0.0
        m["xpre"] = xpre
        m["c_pgw"] = pgw
        m["c_pgb"] = np.ascontiguousarray(pgb.reshape(128, 56))
        m["c_pcw"] = np.ascontiguousarray(pcw.reshape(128, 280))
        m["c_pflags"] = flags
        in_maps.append(m)
    return in_maps


_PROGRAM = {}


def kernel(**inputs):
    in_maps = _prep_inputs(inputs)
    if "nc" not in _PROGRAM:
        _PROGRAM["nc"] = build_program()
    nc = _PROGRAM["nc"]
    res = run_bass_kernel_spmd(nc, in_maps, core_ids=list(range(NCORES)))
    y_prompt = np.zeros((1, 16384, D), np.float32)
    y_sample = np.zeros((32, UT, D), np.float32)
    for c in range(NCORES):
        yu = np.asarray(res.results[c]["yu"]).reshape(NUNITS, UT, D)
        for i in range(4):
            y_sample[4 * c + i] = yu[i]
        y_prompt[0, c * UT:(c + 1) * UT] = yu[4]
    return (y_prompt, y_sample)
```

```python
import contextlib
import numpy as np
import concourse.bass as bass
import concourse.mybir as mybir
from concourse.bass_utils import run_bass_kernel_spmd

F32 = mybir.dt.float32
BF16 = mybir.dt.bfloat16
AF = mybir.ActivationFunctionType
ALU = mybir.AluOpType

D = 1024
D_IN = 11280
NCORES = 8
NUNITS = 5
UT = 2048
XT = 21
XROWS = XT * 128
EPS = 1e-6
NEGB = -200.0
MQ0, MK0, MV0, MO0, MZ0, MG0 = 0, 1024, 2048, 3072, 4096, 5120
NQ0, NK0, NV0, NZ0, GA0, GB0 = 5136, 6160, 7184, 8208, 9232, 10256
JB = (0, 1, 14, 15)


import heapq


class Buf:
    __slots__ = ("name", "w", "r", "dsem", "dn", "last_dma")

    def __init__(self, name):
        self.name = name
        self.w = None
        self.r = []
        self.dsem = None
        self.dn = 0
        self.last_dma = None


class Eng:
    def __init__(self, name, eng, sem):
        self.name, self.eng, self.sem = name, eng, sem
        self.count = 0
        self.waited = {}


class Op:
    __slots__ = ("eng", "calls", "preds", "dur", "idx", "kind", "track", "token", "end", "nsucc", "succs", "dma",
                 "prio")

    def __init__(self):
        self.token = None
        self.end = 0.0
        self.succs = []


class _Dummy:
    def then_inc(self, *a, **k):
        return self


class Rec:
    def __init__(self):
        self.calls = []

    def __getattr__(self, name):
        def f(*args, **kwargs):
            self.calls.append((name, args, kwargs))
            return _Dummy()
        return f


def _fsize(ap):
    n = 1
    for s in list(ap.shape)[1:]:
        n *= int(s)
    return n


def _cost(ename, calls):
    t = 0.0
    for name, args, kw in calls:
        if name == "matmul":
            t += 0.03 + _fsize(kw["rhs"]) / 2000.0
        elif name == "transpose":
            t += 0.1
        elif name == "activation":
            t += 0.22 + _fsize(kw["in_"]) / 1000.0
        elif name == "reciprocal":
            t += 0.15 + 8 * _fsize(kw["in_"]) / 960.0
        elif name == "memset":
            t += 0.1 + _fsize(args[0]) / 960.0
        else:
            ap = kw.get("in0", kw.get("in_", None))
            n = _fsize(ap) if ap is not None else 64
            t += (0.5 + n / 500.0) if ename == "pool" else (0.13 + n / 960.0)
    return t


class Trk:
    HOP = 0.25

    def __init__(self, nc, es):
        self.nc = nc
        self.es = es
        self.E = {}
        self.pending = []
        self.nidx = 0
        self.prio = 1

    def add_engine(self, name, eng):
        sem = self.es.enter_context(self.nc.semaphore("e_" + name))
        self.E[name] = Eng(name, eng, sem)
        return self.E[name]

    def dma_buf(self, name):
        b = Buf(name)
        b.dsem = self.es.enter_context(self.nc.semaphore("d_" + name))
        return b

    def _link(self, op, reads, writes):
        preds = set()
        for b in reads:
            if b.w is not None:
                preds.add(b.w)
        for b in writes:
            if b.w is not None:
                preds.add(b.w)
            for r_ in b.r:
                preds.add(r_)
        preds.discard(op)
        op.preds = list(preds)
        for b in writes:
            b.w = op
            b.r = []
        for b in reads:
            b.r.append(op)
        op.idx = self.nidx
        op.prio = self.prio
        self.nidx += 1
        self.pending.append(op)

    def op(self, ename, fn, reads=(), writes=()):
        rec = Rec()
        fn(rec)
        o = Op()
        o.eng, o.calls, o.kind, o.track = ename, rec.calls, "c", None
        o.dur = _cost(ename, rec.calls)
        self._link(o, reads, writes)
        return o

    def dma(self, qname, out, in_, reads=(), writes=(), track=None):
        o = Op()
        o.eng, o.calls, o.kind, o.track = qname, None, "d", track
        o.dma = (out, in_)
        nbytes = _fsize(out) * 128 * 4
        o.dur = 2.0 + nbytes / 300000.0
        self._link(o, reads, writes)
        if track.last_dma is not None and track.last_dma not in o.preds:
            o.preds.append(track.last_dma)
        track.last_dma = o
        return o

    def flush(self):
        ops = self.pending
        self.pending = []
        if not ops:
            return
        inseg = set(id(o) for o in ops)
        for o in ops:
            o.nsucc = 0
            o.succs = []
        npend = {}
        for o in ops:
            c = 0
            for p in o.preds:
                if id(p) in inseg:
                    p.succs.append(o)
                    c += 1
            npend[id(o)] = c
        ready_t = {}
        free = {e: 0.0 for e in self.E}
        future = {e: [] for e in self.E}
        avail = {e: [] for e in self.E}
        order = {e: [] for e in self.E}

        def make_ready(o):
            t = 0.0
            for p in o.preds:
                if id(p) in inseg:
                    lat = 0.0 if (p.eng == o.eng) else self.HOP
                    t = max(t, p.end + lat)
            heapq.heappush(future[o.eng], (t, o.idx, o))

        for o in ops:
            if npend[id(o)] == 0:
                make_ready(o)
        nleft = len(ops)
        while nleft:
            best = None
            for e in self.E:
                fu, av = future[e], avail[e]
                while fu and fu[0][0] <= free[e]:
                    _, i_, o_ = heapq.heappop(fu)
                    heapq.heappush(av, (o_.prio, i_, o_))
                if av:
                    st = free[e]
                elif fu:
                    st = fu[0][0]
                else:
                    continue
                if best is None or st < best[0]:
                    best = (st, e)
            st, e = best
            if avail[e]:
                _, _, o = heapq.heappop(avail[e])
            else:
                _, _, o = heapq.heappop(future[e])
            if o.kind == "d":
                free[e] = st + 0.08
                o.end = st + o.dur
            else:
                free[e] = st + o.dur
                o.end = free[e]
            order[e].append(o)
            nleft -= 1
            for s_ in o.succs:
                npend[id(s_)] -= 1
                if npend[id(s_)] == 0:
                    make_ready(s_)
        for e, lst in order.items():
            E = self.E[e]
            c = E.count
            for o in lst:
                if o.kind == "c":
                    c += 1
                    o.token = (E.name, E.sem, c)
                else:
                    o.track.dn += 1
                    o.token = ("d_" + o.track.name, o.track.dsem, 16 * o.track.dn)
        for e, lst in order.items():
            E = self.E[e]
            for o in lst:
                deps = {}
                for p in o.preds:
                    k, sem, val = p.token
                    if k == "pe" and e == "pe":
                        continue
                    if k not in deps or deps[k][1] < val:
                        deps[k] = (sem, val)
                for k, (sem, val) in deps.items():
                    if E.waited.get(k, 0) < val:
                        E.eng.wait_ge(sem, val)
                        E.waited[k] = val
                if o.kind == "c":
                    ins = None
                    for name, args, kw in o.calls:
                        ins = getattr(E.eng, name)(*args, **kw)
                    E.count += 1
                    ins.then_inc(E.sem, 1)
                else:
                    ins = E.eng.dma_start(out=o.dma[0], in_=o.dma[1])
                    ins.then_inc(o.track.dsem, 16)
                o.calls = None
                o.dma = None

    def wait_all(self, ename, bufs):
        self.flush()
        E = self.E[ename]
        deps = {}
        for b in bufs:
            for o in ([b.w] if b.w is not None else []) + list(b.r):
                k, sem, val = o.token
                if k not in deps or deps[k][1] < val:
                    deps[k] = (sem, val)
        for k, (sem, val) in deps.items():
            if E.waited.get(k, 0) < val:
                E.eng.wait_ge(sem, val)
                E.waited[k] = val


def build_program(nunits=NUNITS, debug=False, prepass=True):
    nc = bass.Bass("TRN2", target_bir_lowering=False)
    es = contextlib.ExitStack()
    with es:
        tk = Trk(nc, es)
        tk.add_engine("pe", nc.tensor)
        tk.add_engine("act", nc.scalar)
        tk.add_engine("dve", nc.vector)
        tk.add_engine("pool", nc.gpsimd)
        tk.add_engine("sp", nc.sync)

        def dram(name, shape, dt, kind):
            return nc.dram_tensor(name, list(shape), dt, kind=kind).ap()

        xu = dram("xu", [nunits, XROWS, D], F32, "ExternalInput")
        w_in = dram("w_in", [D, D_IN], F32, "ExternalInput")
        w_da = dram("w_da", [D, D], F32, "ExternalInput")
        w_db = dram("w_db", [D, D], F32, "ExternalInput")
        w_out = dram("w_out", [D, D], F32, "ExternalInput")
        c_normT = dram("c_normT", [128, 8], F32, "ExternalInput")
        c_normb = dram("c_normb", [128, D], F32, "ExternalInput")
        c_cw = dram("c_cw", [128, 16 * 5], F32, "ExternalInput")
        c_cb = dram("c_cb", [128, 16], F32, "ExternalInput")
        c_bg = dram("c_bg", [128, 16], F32, "ExternalInput")
        c_mhw = dram("c_mhw", [128, D], F32, "ExternalInput")
        c_fnw = dram("c_fnw", [128, D], F32, "ExternalInput")
        c_mask = dram("c_mask", [128, 3 * 128], F32, "ExternalInput")
        c_ident = dram("c_ident", [128, 128], F32, "ExternalInput")
        c_nai = dram("c_nai", [128, 16 * 5 * 128], F32, "ExternalInput")
        c_nab = dram("c_nab", [2 * 4 * 8, 128, 2 * 5 * 128], F32, "ExternalInput")
        if prepass:
            xpre = dram("xpre", [7, 17 * 128, D], F32, "ExternalInput")
            c_pgw = dram("c_pgw", [7, D, 8], F32, "ExternalInput")
            c_pgb = dram("c_pgb", [128, 56], F32, "ExternalInput")
            c_pcw = dram("c_pcw", [128, 7 * 40], F32, "ExternalInput")
            c_pflags = dram("c_pflags", [128, 16], F32, "ExternalInput")
            sc_st = dram("sc_st", [2, 4, 128, 514], F32, "Internal")
        yu = dram("yu", [nunits, UT, D], F32, "ExternalOutput")
        sc_ya = dram("sc_ya", [nunits, 128, 8, UT], BF16, "Internal")
        sc_yb = dram("sc_yb", [nunits, 128, 8, UT], BF16, "Internal")
        dbg = {}
        if debug:
            dbg["xnT"] = dram("dbg_xnT", [128, 1024], F32, "ExternalOutput")
            dbg["gates"] = dram("dbg_gates", [128, 384], F32, "ExternalOutput")

        def sb(name, shape, dt):
            return es.enter_context(nc.sbuf_tensor(name, list(shape), dt))

        def psum(name, shape, dt):
            return es.enter_context(nc.psum_tensor(name, list(shape), dt))

        NPF = 6
        psf = [psum("psf%d" % i, [128, 512], F32) for i in range(NPF)]
        psf_b = [Buf("psf%d" % i) for i in range(NPF)]
        psb = [psum("psb%d" % i, [128, 1024], BF16) for i in range(2)]
        psb_b = [Buf("psb%d" % i) for i in range(2)]
        rr = {"f": 0, "b": 0, "n": NPF, "g": 0, "split": False}

        def nextf():
            i = rr["f"] % rr["n"]
            rr["f"] += 1
            return psf[i], psf_b[i]

        def nextg():
            if not rr.get("split"):
                return nextf()
            i = 4 + rr["g"] % 2
            rr["g"] += 1
            return psf[i], psf_b[i]

        def nextb():
            i = rr["b"] % 2
            rr["b"] += 1
            return psb[i], psb_b[i]

        cst = tk.dma_buf("cst")
        normT = sb("normT", [128, 8], F32)
        cw = sb("cw", [128, 80], F32)
        cb = sb("cb", [128, 16], F32)
        bg = sb("bg", [128, 16], F32)
        mask = sb("mask", [128, 384], F32)
        normb = sb("normb", [128, D], F32)
        ident_f = sb("ident_f", [128, 128], F32)
        ident = sb("ident", [128, 128], BF16)
        ident_b = Buf("ident")
        col = sb("col", [128, 4], F32)
        col_b = Buf("col")
        for dst, src in ((normT, c_normT), (cw, c_cw), (cb, c_cb), (bg, c_bg), (mask, c_mask), (ident_f, c_ident),
                         (normb, c_normb)):
            tk.dma("sp", dst[:], src[:, :], writes=[cst], track=cst)
        tk.op("dve", lambda g: g.tensor_copy(out=ident[:], in_=ident_f[:]), reads=[cst], writes=[ident_b])
        tk.op("dve", lambda g: g.memset(col[:, 0:1], 1.0), writes=[col_b])
        tk.op("dve", lambda g: g.memset(col[:, 1:2], -0.5), writes=[col_b])
        tk.op("dve", lambda g: g.memset(col[:, 2:3], -float(np.log(16.0))), writes=[col_b])
        tk.op("dve", lambda g: g.memset(col[:, 3:4], -1.0), writes=[col_b])
        cbh = sb("cbh", [128, 16], F32)
        cbh_b = Buf("cbh")
        tk.op("dve", lambda g: g.tensor_scalar(out=cbh[:], in0=cb[:], scalar1=0.5, scalar2=None, op0=ALU.mult),
              reads=[cst], writes=[cbh_b])
        tht = [sb("tht%d" % i, [128, 512], F32) for i in range(2)]
        tht_b = [Buf("tht%d" % i) for i in range(2)]
        mask_f = mask[:, 0:128]
        mask_b = mask[:, 128:256]
        ones32 = mask[:, 256:384]

        xnT = sb("xnT", [128, 8, XROWS], BF16)
        xnT_b = [Buf("xnT%d" % e) for e in range(XT)]
        xs = [sb("xs%d" % i, [128, D], F32) for i in range(2)]
        xs_b = [tk.dma_buf("xs%d" % i) for i in range(2)]
        xb = [sb("xb%d" % i, [128, D], BF16) for i in range(2)]
        xb_b = [Buf("xb%d" % i) for i in range(2)]
        junk = sb("junk", [128, D], BF16)
        junk_b = Buf("junk")
        sm = [sb("sm%d" % i, [128, 8], F32) for i in range(4)]
        sm_b = [Buf("sm%d" % i) for i in range(4)]
        smr = {"i": 0}

        def nextsm():
            i = smr["i"] % 4
            smr["i"] += 1
            return sm[i], sm_b[i]

        ev = sb("ev", [128, 16, 2, 4], F32)
        rsv = sb("rsv", [128, 16, 2, 4], F32)
        av = sb("av", [128, 16, 2, 4], F32)
        nrsv = sb("nrsv", [128, 16, 2, 4], F32)
        hrsv = sb("hrsv", [128, 16, 2, 4], F32)
        gate_b = [Buf("gate%d" % t) for t in range(16)]
        gtmp = [sb("gtmp%d" % i, [128, 2, 8], F32) for i in range(2)]
        gtmp2 = [sb("gtmpb%d" % i, [128, 3, 2, 4], F32) for i in range(2)]
        gtmp_b = [Buf("gtmp%d" % i) for i in range(2)]

        NW = 6
        wr = [sb("wr%d" % i, [128, 8, 256], BF16) for i in range(NW)]
        wr_b = [tk.dma_buf("wr%d" % i) for i in range(NW)]
        wrr = {"i": 0}

        def load_w(wap, c0, ncols=256):
            i = wrr["i"] % NW
            wrr["i"] += 1
            src = wap[:, c0:c0 + ncols].rearrange("(f p) c -> p f c", p=128)
            tk.dma("pool", wr[i][:, :, 0:ncols], src, writes=[wr_b[i]], track=wr_b[i])
            return wr[i], wr_b[i]

        yst = [sb("yst%d" % i, [128, 2, 128], BF16) for i in range(4)]
        yst_b = [tk.dma_buf("yst%d" % i) for i in range(4)]
        ystr = {"i": 0}
        scya_b = [[Buf("scya%d_%d" % (u, t)) for t in range(16)] for u in range(nunits)]
        scyb_b = [[Buf("scyb%d_%d" % (u, t)) for t in range(16)] for u in range(nunits)]
        nstage_b = [tk.dma_buf("nstage%d" % i) for i in range(2)]
        ytl_b = [tk.dma_buf("ytl%d" % i) for i in range(2)]
        res_b = [tk.dma_buf("res%d" % i) for i in range(2)]
        pcst_b = tk.dma_buf("pcst")
        dbg_b = []

        def barrier(dbufs=()):
            tk.flush()
            for e in tk.E.values():
                for f_ in tk.E.values():
                    if f_ is e or f_.count == 0:
                        continue
                    if e.waited.get(f_.name, 0) < f_.count:
                        e.eng.wait_ge(f_.sem, f_.count)
                        e.waited[f_.name] = f_.count
                for b_ in dbufs:
                    k = "d_" + b_.name
                    if b_.dn and e.waited.get(k, 0) < 16 * b_.dn:
                        e.eng.wait_ge(b_.dsem, 16 * b_.dn)
                        e.waited[k] = 16 * b_.dn

        def mm_group(writes, reads, fn):
            return tk.op("pe", fn, reads=reads, writes=writes)

        uid = {"n": 0}

        def phase_alloc(ph):
            uid["n"] += 1
            tag = "_p%d" % uid["n"]

            def sbp(name, shape, dt):
                return ph.enter_context(nc.sbuf_tensor(name + tag, list(shape), dt))
            return sbp

        def run(g_):
            for _ in g_:
                pass

        def chain_g(*gens):
            for g_ in gens:
                yield from g_

        def adv(g_, k):
            if g_ is None:
                return
            for _ in range(k):
                try:
                    next(g_)
                except StopIteration:
                    return

        def stage0(src_fn, tiles):
            for st_, e in tiles:
                s = e % 2
                tk.dma("sp", xs[s][:], src_fn(st_), writes=[xs_b[s]], track=xs_b[s])
                c1, c1b = nextsm()
                tk.op("act", lambda g: g.activation(out=junk[:], in_=xs[s][:], func=AF.Square, accum_out=c1[:, 0:1]),
                      reads=[xs_b[s]], writes=[junk_b, c1b])
                tk.op("dve", lambda g: g.tensor_scalar(out=c1[:, 1:2], in0=c1[:, 0:1], scalar1=1.0 / D, scalar2=EPS,
                                                       op0=ALU.mult, op1=ALU.add), reads=[c1b], writes=[c1b])
                tk.op("pool", lambda g: g.tensor_tensor(out=c1[:, 2:3], in0=c1[:, 1:2], in1=col[:, 1:2], op=ALU.pow),
                      reads=[c1b, col_b], writes=[c1b])
                tk.op("dve", lambda g: g.scalar_tensor_tensor(out=xb[s][:], in0=xs[s][:], scalar=c1[:, 2:3],
                                                              in1=normb[:, :], op0=ALU.mult, op1=ALU.mult),
                      reads=[xs_b[s], c1b, cst], writes=[xb_b[s]])
                pb, pbb = nextb()

                def f(g):
                    last = None
                    for fc in range(8):
                        last = g.transpose(out=pb[:, fc * 128:(fc + 1) * 128], in_=xb[s][:, fc * 128:(fc + 1) * 128],
                                           identity=ident[:])
                    return last
                mm_group([pbb], [xb_b[s], ident_b], f)
                if e % 2 == 0:
                    tk.op("act", lambda g: g.activation(out=xnT[:, :, e * 128:(e + 1) * 128],
                                                        in_=pb[:, 0:1024].rearrange("p (f t) -> p f t", f=8),
                                                        func=AF.Copy), reads=[pbb], writes=[xnT_b[e]])
                else:
                    tk.op("dve", lambda g: g.tensor_copy(out=xnT[:, :, e * 128:(e + 1) * 128],
                                                         in_=pb[:, 0:1024].rearrange("p (f t) -> p f t", f=8)),
                          reads=[pbb], writes=[xnT_b[e]])
                yield

        def qk_proj(B, h, qk, cwt, cwdeps, ch_off, parts=("mm", "conv"), silu=False):
            dstT, dst_b = (B['qT'], B['qT_b']) if qk == 0 else (B['kT'], B['kT_b'])
            if "mm" in parts:
                wq, wqb = load_w(w_in, (MQ0 if qk == 0 else MK0) + h * 256)
                for m in range(2):
                    ch = qk * 8 + h * 2 + m
                    ch_ = ch - ch_off
                    ps_ = (qk * 2 + m) % 2
                    pr, prb = B['pre'][m], B['pre_b'][m]
                    for tb in range(4):
                        pp, ppb = nextg()

                        def f(g):
                            last = None
                            for fc in range(8):
                                last = g.matmul(pp[:, 0:512], lhsT=wq[:, fc, m * 128:(m + 1) * 128],
                                                rhs=xnT[:, fc, 256 + tb * 512:256 + (tb + 1) * 512],
                                                start=(fc == 0), stop=(fc == 7))
                            return last
                        mm_group([ppb], [wqb] + xnT_b[2 + tb * 4:2 + tb * 4 + 4], f)
                        tk.op("act", lambda g: g.activation(out=pr[:, 2 + tb * 512:2 + (tb + 1) * 512],
                                                            in_=pp[:, 0:512], func=AF.Identity, scale=0.5),
                              reads=[ppb], writes=[prb])
                        yield
                    pp, ppb = nextg()

                    def f(g):
                        last = None
                        for fc in range(8):
                            last = g.matmul(pp[:, 0:4], lhsT=wq[:, fc, m * 128:(m + 1) * 128],
                                            rhs=xnT[:, fc, 2560:2564], start=(fc == 0), stop=(fc == 7))
                        return last
                    mm_group([ppb], [wqb, xnT_b[20]], f)
                    tk.op("dve", lambda g: g.tensor_scalar(out=pr[:, 0:2], in0=pp[:, 0:2], scalar1=0.5, scalar2=None,
                                                           op0=ALU.mult), reads=[ppb], writes=[prb])
                    tk.op("dve", lambda g: g.tensor_scalar(out=pr[:, UT + 2:UT + 4], in0=pp[:, 2:4], scalar1=0.5,
                                                           scalar2=None, op0=ALU.mult), reads=[ppb], writes=[prb])
            if "conv" in parts:
                for m in range(2):
                    ch = qk * 8 + h * 2 + m
                    ch_ = ch - ch_off
                    ps_ = (qk * 2 + m) % 2
                    pr, prb = B['pre'][m], B['pre_b'][m]
                    for tb in range(4):
                        a_ = tb % 2
                        ac, acb = B['acc'][a_], B['acc_b'][a_]
                        tk.op("act", lambda g: g.activation(out=ac[:], in_=pr[:, tb * 512:tb * 512 + 512],
                                                            func=AF.Identity, scale=cwt[:, ch_ * 5:ch_ * 5 + 1],
                                                            bias=cbh[:, ch:ch + 1]),
                              reads=[prb, cst, cbh_b] + cwdeps, writes=[acb])
                        for i in range(1, 5):
                            tk.op("dve", lambda g, i=i: g.scalar_tensor_tensor(
                                out=ac[:], in0=pr[:, tb * 512 + i:tb * 512 + i + 512],
                                scalar=cwt[:, ch_ * 5 + i:ch_ * 5 + i + 1], in1=ac[:], op0=ALU.mult, op1=ALU.add),
                                reads=[prb, cst, acb] + cwdeps, writes=[acb])
                        if silu:
                            tk.op("act", lambda g: g.activation(out=dstT[:, m, tb * 512:(tb + 1) * 512], in_=ac[:],
                                                                func=AF.Silu, scale=2.0),
                                  reads=[acb], writes=[dst_b[tb]])
                        else:
                            th_, th_b = tht[a_], tht_b[a_]
                            tk.op("act", lambda g: g.activation(out=th_[:], in_=ac[:], func=AF.Tanh),
                                  reads=[acb], writes=[th_b])
                            tk.op("dve", lambda g: g.scalar_tensor_tensor(out=dstT[:, m, tb * 512:(tb + 1) * 512],
                                                                          in0=th_[:], scalar=1.0, in1=ac[:],
                                                                          op0=ALU.add, op1=ALU.mult),
                                  reads=[acb, th_b], writes=[dst_b[tb]])
                        yield


        def v_proj(B, h, sc=None):
            wv, wvb = load_w(w_in, MV0 + h * 256)
            for t in range(16):
                e = t + 2
                pp, ppb = nextg()

                def f(g):
                    last = None
                    for fc in range(8):
                        last = g.matmul(pp[:, 0:256], lhsT=xnT[:, fc, e * 128:(e + 1) * 128], rhs=wv[:, fc, :],
                                        start=(fc == 0), stop=(fc == 7))
                    return last
                mm_group([ppb], [wvb, xnT_b[e]], f)
                if sc is None:
                    tk.op("act", lambda g: g.activation(out=B['vaug'][:, t, 0:256], in_=pp[:, 0:256], func=AF.Copy),
                          reads=[ppb], writes=[B['vaug_b'][t]])
                else:
                    tk.op("act", lambda g: g.activation(out=B['vaug'][:, t, 0:256], in_=pp[:, 0:256],
                                                        func=AF.Identity, scale=sc[0][:, t, h:h + 1]),
                          reads=[ppb, sc[1]], writes=[B['vaug_b'][t]])
                yield

        def k_trans(B, evac_eng="act"):
            for t in range(16):
                pb, pbb = nextb()

                def f(g):
                    g.transpose(out=pb[:, 0:128], in_=B['kT'][:, 0, t * 128:(t + 1) * 128], identity=ident[:])
                    return g.transpose(out=pb[:, 128:256], in_=B['kT'][:, 1, t * 128:(t + 1) * 128], identity=ident[:])
                mm_group([pbb], [B['kT_b'][t // 4], ident_b], f)
                if evac_eng == "act":
                    tk.op("act", lambda g: g.activation(out=B['ktok'][:, t, :], in_=pb[:, 0:256], func=AF.Copy),
                          reads=[pbb], writes=[B['ktok_b'][t]])
                else:
                    tk.op("dve", lambda g: g.tensor_copy(out=B['ktok'][:, t, :], in_=pb[:, 0:256]),
                          reads=[pbb], writes=[B['ktok_b'][t]])
                yield

        tinit_b = [tk.dma_buf("tinit%d" % i) for i in range(2)]
        scst_b = [[Buf("scst%d_%d" % (d_, h)) for h in range(4)] for d_ in range(2)]
        if prepass:
            ph = contextlib.ExitStack()
            sbp = phase_alloc(ph)
            PBs = []
            for k_ in range(2):
                PBs.append(dict(
                    pre=[sbp("pre%d_%d" % (k_, i), [128, UT + 4], F32) for i in range(2)],
                    pre_b=[Buf("pre%d_%d" % (k_, i)) for i in range(2)],
                    acc=[sbp("acc%d_%d" % (k_, i), [128, 512], F32) for i in range(2)],
                    acc_b=[Buf("acc%d_%d" % (k_, i)) for i in range(2)],
                    kT=sbp("kT%d" % k_, [128, 2, UT], BF16), kT_b=[Buf("kT%d_%d" % (k_, i)) for i in range(4)],
                    ktok=sbp("ktok%d" % k_, [128, 16, 256], BF16),
                    ktok_b=[Buf("ktok%d_%d" % (k_, i)) for i in range(16)],
                    vaug=sbp("vaug%d" % k_, [128, 16, 257], BF16),
                    vaug_b=[Buf("vaug%d_%d" % (k_, i)) for i in range(16)]))
                tk.op("dve", lambda g: g.memset(PBs[k_]["vaug"][:, :, 256:257], 1.0), writes=PBs[k_]["vaug_b"])
            evp2 = [sbp("evp%d" % k_, [128, 16, 4], F32) for k_ in range(2)]
            avp2 = [sbp("avp%d" % k_, [128, 16, 4], F32) for k_ in range(2)]
            gp_b2 = [[Buf("gp%d_%d" % (k_, t)) for t in range(16)] for k_ in range(2)]
            Tp = [sbp("Tp%d" % h, [128, 2, 257], F32) for h in range(4)]
            Tp_b = [Buf("Tp%d" % h) for h in range(4)]
            SF = [sbp("SF%d" % h, [128, 2, 257], F32) for h in range(4)]
            SF_b = [Buf("SF%d" % h) for h in range(4)]
            alast = sbp("alast", [128, 4], F32)
            alast_b = Buf("alast")
            flags = sbp("flags", [128, 16], F32)
            cwp = sbp("cwp", [128, 7, 40], F32)
            pgb = sbp("pgb", [128, 7, 8], F32)
            pvt = [sbp("pvt%d" % i, [128, 257], BF16) for i in range(3)]
            pvt_b = [Buf("pvt%d" % i) for i in range(3)]
            pst_b = tk.dma_buf("pst")
            tk.dma("sp", flags[:], c_pflags[:, :], writes=[pcst_b], track=pcst_b)
            tk.dma("sp", cwp[:].rearrange("p a b -> p (a b)"), c_pcw[:, :], writes=[pcst_b], track=pcst_b)
            tk.dma("sp", pgb[:].rearrange("p a b -> p (a b)"), c_pgb[:, :], writes=[pcst_b], track=pcst_b)
            for h in range(4):
                tk.op("dve", lambda g: g.memset(Tp[h][:], 0.0), writes=[Tp_b[h]])
                tk.op("dve", lambda g: g.memset(SF[h][:], 0.0), writes=[SF_b[h]])
            tk.op("dve", lambda g: g.memset(alast[:], 1.0), writes=[alast_b])

            xs1 = [sbp("xs1_%d" % k_, [128, 16, 4], F32) for k_ in range(2)]
            tots = [sbp("tots%d" % k_, [128, 16, 4], F32) for k_ in range(2)]
            Rv = [sbp("Rv%d" % k_, [128, 16, 4], F32) for k_ in range(2)]
            aslot2 = [sbp("aslot%d" % k_, [128, 8], F32) for k_ in range(2)]
            gs_b = [Buf("gs%d" % k_) for k_ in range(2)]

            def pre_slot_front(i):
                k_ = i % 2
                evp, gsb = evp2[k_], gs_b[k_]
                yield from stage0(lambda st_: xpre[i, st_ * 128:(st_ + 1) * 128, :],
                                  [(t, t + 2) for t in range(16)] + [(16, 20)])
                wg, wgb = load_w(c_pgw[i], 0, 8)
                for t in range(16):
                    e = t + 2
                    s = t % 2
                    pg, pgb_ = nextf()

                    def f(g):
                        last = None
                        for fc in range(8):
                            last = g.matmul(pg[:, 0:8], lhsT=xnT[:, fc, e * 128:(e + 1) * 128], rhs=wg[:, fc, 0:8],
                                            start=(fc == 0), stop=(fc == 7))
                        return last
                    mm_group([pgb_], [xnT_b[e], wgb], f)
                    g3 = gtmp[s][:, 0, :]
                    g2 = gtmp2[s]
                    tk.op("dve", lambda g: g.tensor_tensor(out=g3, in0=pg[:, 0:8], in1=pgb[:, i, :], op=ALU.add),
                          reads=[pgb_, pcst_b], writes=[gtmp_b[s]])
                    tk.op("act", lambda g: g.activation(out=g2[:, 0, 0, :], in_=gtmp[s][:, 0, 4:8], func=AF.Exp,
                                                        scale=-1.0), reads=[gtmp_b[s]], writes=[gtmp_b[s]])
                    tk.op("act", lambda g: g.activation(out=g2[:, 1, 0, :], in_=g2[:, 0, 0, :], func=AF.Ln,
                                                        bias=col[:, 0:1], scale=1.0),
                          reads=[gtmp_b[s], col_b], writes=[gtmp_b[s]])
                    pc, pcb = nextf()

                    def f(g):
                        g.matmul(pc[:, 0:4], lhsT=mask_f, rhs=g2[:, 1, 0, :], start=True, stop=True)
                        return g.matmul(pc[:, 4:8], lhsT=ones32, rhs=g2[:, 1, 0, :], start=True, stop=True)
                    mm_group([pcb], [gtmp_b[s], cst], f)
                    tk.op("dve", lambda g: g.tensor_tensor(out=g2[:, 2, 0, :], in0=pc[:, 0:4],
                                                           in1=gtmp[s][:, 0, 0:4], op=ALU.add),
                          reads=[pcb, gtmp_b[s]], writes=[gtmp_b[s]])
                    tk.op("dve", lambda g: g.tensor_tensor(out=xs1[k_][:, t, :], in0=g2[:, 2, 0, :], in1=pc[:, 4:8],
                                                           op=ALU.subtract),
                          reads=[pcb, gtmp_b[s]], writes=[gsb])
                    tk.op("act", lambda g: g.activation(out=tots[k_][:, t, :], in_=pc[:, 4:8], func=AF.Copy),
                          reads=[pcb], writes=[gsb])
                    yield
                tk.op("dve", lambda g: g.memset(Rv[k_][:, 15, :], 0.0), writes=[gsb])
                for t in range(14, -1, -1):
                    tk.op("dve", lambda g, t=t: g.tensor_tensor(out=Rv[k_][:, t, :], in0=Rv[k_][:, t + 1, :],
                                                                in1=tots[k_][:, t + 1, :], op=ALU.add),
                          reads=[gsb], writes=[gsb])
                tk.op("dve", lambda g: g.tensor_tensor(out=xs1[k_][:].rearrange("p a b -> p (a b)"),
                                                       in0=xs1[k_][:].rearrange("p a b -> p (a b)"),
                                                       in1=Rv[k_][:].rearrange("p a b -> p (a b)"), op=ALU.subtract),
                      reads=[gsb], writes=[gsb])
                tk.op("act", lambda g: g.activation(out=evp[:].rearrange("p a b -> p (a b)"),
                                                    in_=xs1[k_][:].rearrange("p a b -> p (a b)"), func=AF.Exp,
                                                    bias=col[:, 2:3], scale=1.0),
                      reads=[gsb, col_b], writes=[gsb])
                tk.op("dve", lambda g: g.tensor_tensor(out=aslot2[k_][:, 4:8], in0=Rv[k_][:, 0, :],
                                                       in1=tots[k_][:, 0, :], op=ALU.add), reads=[gsb], writes=[gsb])
                tk.op("act", lambda g: g.activation(out=aslot2[k_][:, 0:4], in_=aslot2[k_][:, 4:8], func=AF.Exp,
                                                    scale=-1.0), reads=[gsb], writes=[gsb])

            pitems = [(i, h) for i in range(7) for h in range(4)]

            def pre_front(n):
                i, h = pitems[n]
                if h == 0:
                    yield from pre_slot_front(i)
                yield from qk_proj(PBs[n % 2], h, 1, cwp[:, i, :], [pcst_b], 8, parts=("mm",))
                yield from v_proj(PBs[n % 2], h, sc=(evp2[i % 2], gs_b[i % 2]))
                tk.op("dve", lambda g: g.tensor_copy(out=PBs[n % 2]["vaug"][:, :, 256], in_=evp2[i % 2][:, :, h]),
                      reads=[gs_b[i % 2]], writes=PBs[n % 2]["vaug_b"])
                yield from qk_proj(PBs[n % 2], h, 1, cwp[:, i, :], [pcst_b], 8, parts=("conv",))

            def pre_chain(n, bgen):
                i, h = pitems[n]
                PB = PBs[n % 2]
                k_ = i % 2
                evp, gsb = evp2[k_], gs_b[k_]
                run(k_trans(PB, "dve"))
                pd0, pd0b = psf[4], psf_b[4]
                pd1, pd1b = psf[5], psf_b[5]
                for c in range(16):
                    adv(bgen, 5)
                    v_, v_b = PB["vaug"][:, c, :], PB["vaug_b"][c]

                    def f(g):
                        g.matmul(pd0[:, 0:257], lhsT=PB["ktok"][:, c, 0:128], rhs=v_, start=(c == 0),
                                 stop=(c == 15))
                        return g.matmul(pd1[:, 0:257], lhsT=PB["ktok"][:, c, 128:256], rhs=v_,
                                        start=(c == 0), stop=(c == 15))
                    mm_group([pd0b, pd1b], [PB["ktok_b"][c], v_b], f)
                c1, c1b = nextsm()
                tk.op("dve", lambda g: g.tensor_tensor(out=c1[:, 0:1], in0=aslot2[k_][:, h:h + 1],
                                                       in1=flags[:, i:i + 1], op=ALU.mult),
                      reads=[gsb, pcst_b], writes=[c1b])
                for m, (pd, pdb) in enumerate(((pd0, pd0b), (pd1, pd1b))):
                    tk.op("dve", lambda g, m=m, pd=pd: g.scalar_tensor_tensor(
                        out=Tp[h][:, m, :], in0=Tp[h][:, m, :], scalar=c1[:, 0:1], in1=pd[:, 0:257],
                        op0=ALU.mult, op1=ALU.add), reads=[pdb, Tp_b[h], c1b], writes=[Tp_b[h]])
                tk.op("dve", lambda g: g.scalar_tensor_tensor(
                    out=SF[h][:].rearrange("p a b -> p (a b)"), in0=Tp[h][:].rearrange("p a b -> p (a b)"),
                    scalar=flags[:, 7 + i:8 + i], in1=SF[h][:].rearrange("p a b -> p (a b)"),
                    op0=ALU.mult, op1=ALU.add),
                    reads=[Tp_b[h], SF_b[h], pcst_b], writes=[SF_b[h]])

            rr["n"] = 4
            run(pre_front(0))
            for n in range(len(pitems)):
                bgen = pre_front(n + 1) if n + 1 < len(pitems) else None
                pre_chain(n, bgen)
                if bgen is not None:
                    run(bgen)
            for h in range(4):
                c1, c1b = nextsm()
                tk.op("dve", lambda g: g.tensor_scalar(out=Tp[h][:].rearrange("p a b -> p (a b)"),
                                                       in0=Tp[h][:].rearrange("p a b -> p (a b)"),
                                                       scalar1=flags[:, 14:15], scalar2=None, op0=ALU.mult),
                      reads=[Tp_b[h], pcst_b], writes=[Tp_b[h]])
                tk.dma("sp", sc_st[0, h, :, :], SF[h][:].rearrange("p a b -> p (a b)"), reads=[SF_b[h]],
                       writes=[scst_b[0][h]], track=pst_b)
                tk.dma("sp", sc_st[1, h, :, :], Tp[h][:].rearrange("p a b -> p (a b)"), reads=[Tp_b[h]],
                       writes=[scst_b[1][h]], track=pst_b)
            for d_ in range(2):
                for h in range(4):
                    scst_b[d_][h].w = pst_b.last_dma
            barrier([pst_b, pcst_b])
            ph.close()
            rr["n"] = NPF

        for u in range(nunits):
            utype = 1 if u == 4 else 0
            run(stage0(lambda st_: xu[u, st_ * 128:(st_ + 1) * 128, :], [(e, e) for e in range(XT)]))
            if debug and u == 0:
                dt_ = sb("dbgt", [128, 1024], F32)
                dtb = tk.dma_buf("dbgt")
                dbg_b.append(dtb)
                tk.op("dve", lambda g: g.tensor_copy(out=dt_[:].rearrange("p (f t) -> p f t", f=8),
                                                     in_=xnT[:, :, 256:384]), reads=xnT_b, writes=[dtb])
                tk.dma("sp", dbg["xnT"][:, :], dt_[:], reads=[dtb], track=dtb)

            def xcols(e0, n):
                return slice(e0 * 128, e0 * 128 + n)

            wg, wgb = load_w(w_in, MG0, 16)
            for t in range(16):
                e = t + 2
                pg, pgb = nextf()

                def f(g):
                    last = None
                    for fc in range(8):
                        last = g.matmul(pg[:, 0:16], lhsT=xnT[:, fc, e * 128:(e + 1) * 128], rhs=wg[:, fc, 0:16],
                                        start=(fc == 0), stop=(fc == 7))
                    return last
                mm_group([pgb], [xnT_b[e], wgb], f)
                s = t % 2
                g3 = gtmp[s]
                g2 = gtmp2[s]
                tk.op("dve", lambda g: g.tensor_tensor(out=g3[:].rearrange("p a b -> p (a b)"), in0=pg[:, 0:16],
                                                       in1=bg[:, :], op=ALU.add),
                      reads=[pgb, cst], writes=[gtmp_b[s]])
                tk.op("act", lambda g: g.activation(out=g2[:, 0, :, :], in_=g3[:, :, 4:8], func=AF.Exp, scale=-1.0),
                      reads=[gtmp_b[s]], writes=[gtmp_b[s]])
                tk.op("act", lambda g: g.activation(out=g2[:, 1, :, :], in_=g2[:, 0, :, :], func=AF.Ln,
                                                    bias=col[:, 0:1], scale=1.0),
                      reads=[gtmp_b[s], col_b], writes=[gtmp_b[s]])
                pc, pcb = nextf()

                def f(g):
                    g.matmul(pc[:, 0:4], lhsT=mask_f, rhs=g2[:, 1, 0, :], start=True, stop=True)
                    g.matmul(pc[:, 4:8], lhsT=mask_b, rhs=g2[:, 1, 1, :], start=True, stop=True)
                    return g.matmul(pc[:, 8:16], lhsT=ones32, rhs=g2[:, 1, :, :].rearrange("p a b -> p (a b)"),
                                    start=True, stop=True)
                mm_group([pcb], [gtmp_b[s], cst], f)
                tk.op("dve", lambda g: g.tensor_tensor(out=g2[:, 2, :, :],
                                                       in0=pc[:, 0:8].rearrange("p (a b) -> p a b", a=2),
                                                       in1=g3[:, :, 0:4], op=ALU.add),
                      reads=[pcb, gtmp_b[s]], writes=[gtmp_b[s]])
                tk.op("act", lambda g: g.activation(out=ev[:, t, :, :], in_=g2[:, 2, :, :], func=AF.Exp,
                                                    bias=col[:, 2:3], scale=1.0),
                      reads=[gtmp_b[s], col_b], writes=[gate_b[t]])
                tk.op("act", lambda g: g.activation(out=rsv[:, t, :, :],
                                                    in_=pc[:, 0:8].rearrange("p (a b) -> p a b", a=2),
                                                    func=AF.Exp, scale=-1.0), reads=[pcb], writes=[gate_b[t]])
                tk.op("act", lambda g: g.activation(out=av[:, t, :, :],
                                                    in_=pc[:, 8:16].rearrange("p (a b) -> p a b", a=2),
                                                    func=AF.Exp, scale=-1.0), reads=[pcb], writes=[gate_b[t]])
                tk.op("dve", lambda g: g.tensor_scalar(out=nrsv[:, t, :, :], in0=rsv[:, t, :, :], scalar1=-1.0,
                                                       scalar2=None, op0=ALU.mult),
                      reads=[gate_b[t]], writes=[gate_b[t]])
                tk.op("dve", lambda g: g.tensor_scalar(out=hrsv[:, t, :, :], in0=rsv[:, t, :, :], scalar1=0.5,
                                                       scalar2=None, op0=ALU.mult),
                      reads=[gate_b[t]], writes=[gate_b[t]])
            if debug and u == 0:
                dg = sb("dbgg", [128, 3 * 128], F32)
                dgb = tk.dma_buf("dbgg")
                dbg_b.append(dgb)
                for i, src in enumerate((ev, rsv, av)):
                    tk.op("dve", lambda g, i=i, src=src: g.tensor_copy(
                        out=dg[:, i * 128:(i + 1) * 128], in_=src[:].rearrange("p a b c -> p (a b c)")),
                        reads=gate_b, writes=[dgb])
                tk.dma("sp", dbg["gates"][:, :], dg[:], reads=[dgb], track=dgb)

            ph = contextlib.ExitStack()
            sbp = phase_alloc(ph)
            mhw = sbp("mhw", [128, D], F32)
            tk.dma("sp", mhw[:], c_mhw[:, :], writes=[pcst_b], track=pcst_b)
            acc = [sbp("acc%d" % i, [128, 512], F32) for i in range(2)]
            acc_b = [Buf("acc%d" % i) for i in range(2)]
            MBs = []
            _mpre = [sbp("pre%d" % i, [128, UT + 4], F32) for i in range(2)]
            _mpre_b = [Buf("pre%d" % i) for i in range(2)]
            for k_ in range(2):
                MBs.append(dict(
                    pre=_mpre, pre_b=_mpre_b, acc=acc, acc_b=acc_b,
                    qT=sbp("qT%d" % k_, [128, 2, UT], BF16), qT_b=[Buf("qT%d_%d" % (k_, i)) for i in range(4)],
                    kT=sbp("kT%d" % k_, [128, 2, UT], BF16), kT_b=[Buf("kT%d_%d" % (k_, i)) for i in range(4)],
                    ktok=sbp("ktok%d" % k_, [128, 16, 256], BF16),
                    ktok_b=[Buf("ktok%d_%d" % (k_, i)) for i in range(16)],
                    vaug=sbp("vaug%d" % k_, [128, 16, 257], BF16),
                    vaug_b=[Buf("vaug%d_%d" % (k_, i)) for i in range(16)]))
                tk.op("dve", lambda g: g.memset(MBs[k_]["vaug"][:, :, 256:257], 1.0), writes=MBs[k_]["vaug_b"])
            hpart = sbp("hpart", [128, 16, 256], BF16)
            hpart_b = [Buf("hpart%d" % i) for i in range(16)]
            Tst = [sbp("Tst%d" % i, [128, 2, 257], F32) for i in range(2)]
            Tst_b = [Buf("Tst%d" % i) for i in range(2)]
            Sbf = [sbp("Sbf%d" % i, [128, 2, 257], BF16) for i in range(2)]
            Sbf_b = [Buf("Sbf%d" % i) for i in range(2)]
            vt = [sbp("vt%d" % i, [128, 257], BF16) for i in range(4)]
            vt_b = [Buf("vt%d" % i) for i in range(4)]
            PT = [sbp("PT%d" % i, [128, 128], BF16) for i in range(4)]
            PT_b = [Buf("PT%d" % i) for i in range(4)]
            _fin = [sbp("fin%d" % j, [128, 512 if j == 1 else 256], F32) for j in range(4)]
            _finb = [Buf("fin%d" % j) for j in range(4)]
            fin, fin_b = [_fin, _fin], [_finb, _finb]
            yab = [sbp("yab%d" % i, [128, 256], BF16) for i in range(2)]
            yab_b = [Buf("yab%d" % i) for i in range(2)]

            def prep_gen(h_):
                B_ = MBs[h_ % 2]
                yield from qk_proj(B_, h_, 1, cw, [], 0, parts=("mm",))
                yield from v_proj(B_, h_)
                yield from qk_proj(B_, h_, 1, cw, [], 0, parts=("conv",), silu=True)
                yield from qk_proj(B_, h_, 0, cw, [], 0, silu=True)
                yield from k_trans(B_)

            run(prep_gen(0))
            rr["split"], rr["n"] = True, 4
            for h in range(4):
                MB = MBs[h % 2]
                qT, kT, ktok, vaug = MB["qT"], MB["kT"], MB["ktok"], MB["vaug"]
                qT_b, kT_b, ktok_b, vaug_b = MB["qT_b"], MB["kT_b"], MB["ktok_b"], MB["vaug_b"]
                bgen = prep_gen(h + 1) if h < 3 else None
                wo, wob = load_w(w_in, MO0 + h * 256)
                wz, wzb = load_w(w_in, MZ0 + h * 256)
                for d_ in range(2):
                    if prepass and u == 4:
                        tk.dma("sp", Tst[d_][:].rearrange("p a b -> p (a b)"), sc_st[d_, h, :, :],
                               reads=[scst_b[d_][h]], writes=[Tst_b[d_]], track=tinit_b[d_])
                        tk.op("act", lambda g: g.activation(out=Sbf[d_][:], in_=Tst[d_][:], func=AF.Copy),
                              reads=[Tst_b[d_]], writes=[Sbf_b[d_]])
                    else:
                        tk.op("dve", lambda g: g.memset(Tst[d_][:], 0.0), writes=[Tst_b[d_]])
                        tk.op("dve", lambda g: g.memset(Sbf[d_][:], 0.0), writes=[Sbf_b[d_]])
                vi = 0
                for step in range(16):
                    adv(bgen, 5)
                    for d_ in range(2):
                        c = step if d_ == 0 else 15 - step
                        cprev = c - 1 if d_ == 0 else c + 1
                        cs = slice(c * 128, (c + 1) * 128)
                        v_, v_b = vt[vi % 4], vt_b[vi % 4]
                        P_, P_b = PT[vi % 4], PT_b[vi % 4]
                        vi += 1
                        tk.prio = 0
                        tk.op("act", lambda g: g.activation(out=v_[:], in_=vaug[:, c, :], func=AF.Identity,
                                                            scale=ev[:, c, d_, h:h + 1]),
                              reads=[vaug_b[c], gate_b[c]], writes=[v_b])
                        pn, pnb = nextf()

                        def f(g):
                            g.matmul(pn[:, 257:385], lhsT=kT[:, 0, cs], rhs=qT[:, 0, cs], start=True, stop=False)
                            return g.matmul(pn[:, 257:385], lhsT=kT[:, 1, cs], rhs=qT[:, 1, cs], start=False,
                                            stop=True)
                        mm_group([pnb], [kT_b[c // 4], qT_b[c // 4]], f)
                        mk = mask_f if d_ == 0 else mask_b
                        tk.op("dve", lambda g: g.tensor_tensor(out=P_[:], in0=pn[:, 257:385], in1=mk, op=ALU.mult),
                              reads=[pnb, cst], writes=[P_b])

                        def f(g):
                            g.matmul(pn[:, 0:257], lhsT=qT[:, 0, cs], rhs=Sbf[d_][:, 0, :], start=True, stop=False)
                            g.matmul(pn[:, 0:257], lhsT=qT[:, 1, cs], rhs=Sbf[d_][:, 1, :], start=False, stop=False)
                            return g.matmul(pn[:, 0:257], lhsT=P_[:], rhs=v_[:], start=False, stop=True)
                        mm_group([pnb], [qT_b[c // 4], Sbf_b[d_], P_b, v_b], f)
                        pd0, pd0b = nextf()
                        pd1, pd1b = nextf()

                        def f(g):
                            g.matmul(pd0[:, 0:257], lhsT=ktok[:, c, 0:128], rhs=v_[:], start=True, stop=True)
                            return g.matmul(pd1[:, 0:257], lhsT=ktok[:, c, 128:256], rhs=v_[:], start=True, stop=True)
                        mm_group([pd0b, pd1b], [ktok_b[c], v_b], f)
                        if step == 0:
                            apv, apb = col[:, 0:1], [col_b]
                        else:
                            apv, apb = av[:, cprev, d_, h:h + 1], [gate_b[cprev]]
                        for m, (pd, pdb) in enumerate(((pd0, pd0b), (pd1, pd1b))):
                            tk.op("dve", lambda g, m=m, pd=pd: g.scalar_tensor_tensor(
                                out=Tst[d_][:, m, :], in0=Tst[d_][:, m, :], scalar=apv, in1=pd[:, 0:257],
                                op0=ALU.mult, op1=ALU.add), reads=[pdb, Tst_b[d_]] + apb, writes=[Tst_b[d_]])
                        tk.op("act", lambda g: g.activation(out=Sbf[d_][:], in_=Tst[d_][:], func=AF.Identity,
                                                            scale=av[:, c, d_, h:h + 1]),
                              reads=[Tst_b[d_], gate_b[c]], writes=[Sbf_b[d_]])
                        tk.prio = 1
                        c1, c1b = nextsm()
                        rs_ = rsv[:, c, d_, h:h + 1]
                        tk.op("dve", lambda g: g.tensor_scalar(out=c1[:, 3:4], in0=pn[:, 256:257], scalar1=rs_,
                                                               scalar2=1.0, op0=ALU.mult, op1=ALU.max),
                              reads=[pnb, gate_b[c]], writes=[c1b])
                        tk.op("dve", lambda g: g.scalar_tensor_tensor(out=c1[:, 0:1], in0=pn[:, 256:257],
                                                                      scalar=nrsv[:, c, d_, h:h + 1],
                                                                      in1=c1[:, 3:4], op0=ALU.mult, op1=ALU.max),
                              reads=[pnb, gate_b[c], c1b], writes=[c1b])
                        tk.op("pool", lambda g: g.tensor_tensor(out=c1[:, 1:2], in0=c1[:, 0:1], in1=col[:, 3:4],
                                                                op=ALU.pow), reads=[c1b, col_b], writes=[c1b])
                        tk.op("pool", lambda g: g.tensor_tensor(out=c1[:, 2:3], in0=c1[:, 1:2],
                                                                in1=hrsv[:, c, d_, h:h + 1], op=ALU.mult),
                              reads=[c1b, gate_b[c]], writes=[c1b])
                        if step < 8:
                            tk.op("act", lambda g: g.activation(out=hpart[:, c, :], in_=pn[:, 0:256], func=AF.Identity,
                                                                scale=c1[:, 2:3]), reads=[pnb, c1b],
                                  writes=[hpart_b[c]])
                            continue
                        fs = c % 2
                        f0, f1, f2, f3 = fin[fs]
                        fb0, fb1, fb2, fb3 = fin_b[fs]
                        tk.op("dve", lambda g: g.scalar_tensor_tensor(out=f0[:], in0=pn[:, 0:256], scalar=c1[:, 2:3],
                                                                      in1=hpart[:, c, :], op0=ALU.mult, op1=ALU.add),
                              reads=[pnb, c1b, hpart_b[c]], writes=[fb0])
                        po, pob = nextf()
                        e = c + 2

                        def f(g):
                            last = None
                            for fc in range(8):
                                g.matmul(po[:, 0:256], lhsT=xnT[:, fc, e * 128:(e + 1) * 128], rhs=wo[:, fc, :],
                                         start=(fc == 0), stop=(fc == 7))
                            for fc in range(8):
                                last = g.matmul(po[:, 256:512], lhsT=xnT[:, fc, e * 128:(e + 1) * 128],
                                                rhs=wz[:, fc, :], start=(fc == 0), stop=(fc == 7))
                            return last
                        mm_group([pob], [xnT_b[e], wob, wzb], f)
                        tk.op("act", lambda g: g.activation(out=f1[:, 0:256], in_=po[:, 0:256], func=AF.Tanh,
                                                            scale=0.5), reads=[pob], writes=[fb1])
                        tk.op("act", lambda g: g.activation(out=f2[:], in_=po[:, 256:512], func=AF.Silu),
                              reads=[pob], writes=[fb2])
                        tk.op("dve", lambda g: g.scalar_tensor_tensor(out=f0[:], in0=f1[:, 0:256], scalar=1.0,
                                                                      in1=f0[:], op0=ALU.add, op1=ALU.mult),
                              reads=[fb0, fb1], writes=[fb0])
                        c2, c2b = nextsm()
                        tk.op("act", lambda g: g.activation(out=f3[:], in_=f0[:], func=AF.Square,
                                                            accum_out=c2[:, 0:1]),
                              reads=[fb0], writes=[fb3, c2b])
                        tk.op("dve", lambda g: g.tensor_scalar(out=c2[:, 1:2], in0=c2[:, 0:1], scalar1=1.0 / 256,
                                                               scalar2=EPS, op0=ALU.mult, op1=ALU.add),
                              reads=[c2b], writes=[c2b])
                        tk.op("pool", lambda g: g.tensor_tensor(out=c2[:, 2:3], in0=c2[:, 1:2], in1=col[:, 1:2],
                                                                op=ALU.pow), reads=[c2b, col_b], writes=[c2b])
                        tk.op("dve", lambda g: g.scalar_tensor_tensor(out=f3[:], in0=f0[:], scalar=c2[:, 2:3],
                                                                      in1=mhw[:, h * 256:(h + 1) * 256],
                                                                      op0=ALU.mult, op1=ALU.mult),
                              reads=[fb0, c2b, pcst_b], writes=[fb3])
                        ya_, ya_b = yab[fs], yab_b[fs]
                        tk.op("dve", lambda g: g.tensor_tensor(out=ya_[:], in0=f3[:], in1=f2[:], op=ALU.mult),
                              reads=[fb3, fb2], writes=[ya_b])
                        pb, pbb = nextb()

                        def f(g):
                            g.transpose(out=pb[:, 0:128], in_=ya_[:, 0:128], identity=ident[:])
                            return g.transpose(out=pb[:, 128:256], in_=ya_[:, 128:256], identity=ident[:])
                        mm_group([pbb], [ya_b, ident_b], f)
                        yi = ystr["i"] % 4
                        ystr["i"] += 1
                        tk.op("act", lambda g: g.activation(out=yst[yi][:].rearrange("p a b -> p (a b)"),
                                                            in_=pb[:, 0:256], func=AF.Copy),
                              reads=[pbb], writes=[yst_b[yi]])
                        tk.dma("sp", sc_ya[u, :, 2 * h:2 * h + 2, c * 128:(c + 1) * 128], yst[yi][:],
                               reads=[yst_b[yi]], writes=[scya_b[u][c]], track=yst_b[yi])
                if bgen is not None:
                    run(bgen)

            rr["split"], rr["n"] = False, NPF
            barrier()
            ph.close()
            ph = contextlib.ExitStack()
            sbp = phase_alloc(ph)
            eb_res = sbp("eb_res", [128, 16 * 5 * 128], BF16)
            eb_res_b = Buf("eb_res")
            nstage = [sbp("nstage%d" % i, [128, 1280], F32) for i in range(2)]
            for i in range(8):
                s = i % 2
                tk.dma("sp", nstage[s][:], c_nai[:, i * 1280:(i + 1) * 1280], writes=[nstage_b[s]], track=nstage_b[s])
                tk.op("act", lambda g: g.activation(out=eb_res[:, i * 1280:(i + 1) * 1280], in_=nstage[s][:],
                                                    func=AF.Exp), reads=[nstage_b[s]], writes=[eb_res_b])
            NBs = []
            for k_ in range(2):
                NBs.append(dict(
                    nqT=sbp("nqT%d" % k_, [128, UT], BF16), nqT_b=[Buf("nqT%d_%d" % (k_, i)) for i in range(4)],
                    nkT=sbp("nkT%d" % k_, [128, 2560], BF16), nkT_b=[Buf("nkT%d_%d" % (k_, i)) for i in range(5)],
                    nv=sbp("nv%d" % k_, [128, 20, 2, 65], BF16), nv_b=[Buf("nv%d_%d" % (k_, i)) for i in range(20)],
                    nz=sbp("nz%d" % k_, [128, 16, 128], F32), nz_b=[Buf("nz%d_%d" % (k_, i)) for i in range(16)]))
                tk.op("dve", lambda g: g.memset(NBs[k_]["nv"][:, :, :, 64:65], 2.0), writes=NBs[k_]["nv_b"])
            eb_s = [sbp("eb_s%d" % i, [128, 2, 5, 128], BF16) for i in range(2)]
            eb_s_b = [Buf("eb_s%d" % i) for i in range(2)]
            ex = [sbp("ex%d" % i, [128, 640], BF16) for i in range(2)]
            ex_b = [Buf("ex%d" % i) for i in range(2)]
            et = [sbp("et%d" % i, [128, 640], BF16) for i in range(3)]
            et_b = [Buf("et%d" % i) for i in range(3)]
            ybb = [sbp("ybb%d" % i, [128, 128], BF16) for i in range(2)]
            ybb_b = [Buf("ybb%d" % i) for i in range(2)]
            def na_proj(NB, hp):
                nqT, nkT, nv, nz = NB['nqT'], NB['nkT'], NB['nv'], NB['nz']
                nqT_b, nkT_b, nv_b, nz_b = NB['nqT_b'], NB['nkT_b'], NB['nv_b'], NB['nz_b']
                wq, wqb = load_w(w_in, NQ0 + hp * 128, 128)
                wk, wkb = load_w(w_in, NK0 + hp * 128, 128)
                wv, wvb = load_w(w_in, NV0 + hp * 128, 128)
                wz, wzb = load_w(w_in, NZ0 + hp * 128, 128)
                for tb in range(4):
                    pp, ppb = nextg()

                    def f(g):
                        last = None
                        for fc in range(8):
                            last = g.matmul(pp[:, 0:512], lhsT=wq[:, fc, 0:128],
                                            rhs=xnT[:, fc, 256 + tb * 512:256 + (tb + 1) * 512],
                                            start=(fc == 0), stop=(fc == 7))
                        return last
                    mm_group([ppb], [wqb] + xnT_b[2 + tb * 4:2 + tb * 4 + 4], f)
                    tk.op("act", lambda g: g.activation(out=nqT[:, tb * 512:(tb + 1) * 512], in_=pp[:, 0:512],
                                                        func=AF.Identity, scale=0.125), reads=[ppb], writes=[nqT_b[tb]])
                    yield
                for tb in (range(5) if utype == 1 else ()):
                    pp, ppb = nextg()

                    def f(g):
                        last = None
                        for fc in range(8):
                            last = g.matmul(pp[:, 0:512], lhsT=wk[:, fc, 0:128],
                                            rhs=xnT[:, fc, tb * 512:(tb + 1) * 512],
                                            start=(fc == 0), stop=(fc == 7))
                        return last
                    mm_group([ppb], [wkb] + xnT_b[tb * 4:tb * 4 + 4], f)
                    tk.op("dve", lambda g: g.tensor_copy(out=nkT[:, tb * 512:(tb + 1) * 512], in_=pp[:, 0:512]),
                          reads=[ppb], writes=[nkT_b[tb]])
                    yield
                if utype == 0:
                    for tb in range(4):
                        pp, ppb = nextg()

                        def f(g):
                            last = None
                            for fc in range(8):
                                last = g.matmul(pp[:, 0:512], lhsT=wk[:, fc, 0:128],
                                                rhs=xnT[:, fc, 256 + tb * 512:256 + (tb + 1) * 512],
                                                start=(fc == 0), stop=(fc == 7))
                            return last
                        mm_group([ppb], [wkb] + xnT_b[2 + tb * 4:2 + tb * 4 + 4], f)
                        tk.op("dve", lambda g: g.tensor_copy(out=nkT[:, 256 + tb * 512:256 + (tb + 1) * 512],
                                                             in_=pp[:, 0:512]), reads=[ppb], writes=nkT_b)
                        yield
                    tk.op("act", lambda g: g.activation(out=nkT[:, 0:256], in_=nkT[:, 512:768], func=AF.Copy),
                          reads=nkT_b, writes=nkT_b)
                    tk.op("act", lambda g: g.activation(out=nkT[:, 2304:2560], in_=nkT[:, 1792:2048], func=AF.Copy),
                          reads=nkT_b, writes=nkT_b)
                for e in (range(20) if utype == 1 else range(2, 18)):
                    pp, ppb = nextg()

                    def f(g):
                        last = None
                        for fc in range(8):
                            last = g.matmul(pp[:, 0:128], lhsT=xnT[:, fc, e * 128:(e + 1) * 128], rhs=wv[:, fc, 0:128],
                                            start=(fc == 0), stop=(fc == 7))
                        return last
                    mm_group([ppb], [wvb, xnT_b[e]], f)
                    tk.op("act", lambda g: g.activation(out=nv[:, e, :, 0:64],
                                                        in_=pp[:, 0:128].rearrange("p (a b) -> p a b", a=2),
                                                        func=AF.Copy), reads=[ppb], writes=[nv_b[e]])
                    yield
                if utype == 0:
                    for dst_e, src_e in ((0, 4), (1, 5), (18, 14), (19, 15)):
                        tk.op("dve", lambda g, dst_e=dst_e, src_e=src_e: g.tensor_copy(
                            out=nv[:, dst_e, :, 0:64], in_=nv[:, src_e, :, 0:64]),
                            reads=[nv_b[src_e]], writes=[nv_b[dst_e]])
                for t in range(16):
                    e = t + 2
                    pp, ppb = nextg()

                    def f(g):
                        last = None
                        for fc in range(8):
                            last = g.matmul(pp[:, 0:128], lhsT=xnT[:, fc, e * 128:(e + 1) * 128], rhs=wz[:, fc, 0:128],
                                            start=(fc == 0), stop=(fc == 7))
                        return last
                    mm_group([ppb], [wzb, xnT_b[e]], f)
                    th_, th_b = tht[t % 2], tht_b[t % 2]
                    tk.op("act", lambda g: g.activation(out=th_[:, 0:128], in_=pp[:, 0:128], func=AF.Tanh, scale=0.5),
                          reads=[ppb], writes=[th_b])
                    tk.op("dve", lambda g: g.scalar_tensor_tensor(out=nz[:, t, :], in0=th_[:, 0:128], scalar=1.0,
                                                                  in1=pp[:, 0:128], op0=ALU.add, op1=ALU.mult),
                          reads=[ppb, th_b], writes=[nz_b[t]])
                    yield

            run(na_proj(NBs[0], 0))
            rr["split"], rr["n"] = True, 4
            for hp in range(8):
                NB = NBs[hp % 2]
                nqT, nkT, nv, nz = NB['nqT'], NB['nkT'], NB['nv'], NB['nz']
                nqT_b, nkT_b, nv_b, nz_b = NB['nqT_b'], NB['nkT_b'], NB['nv_b'], NB['nz_b']
                bgen = na_proj(NBs[(hp + 1) % 2], hp + 1) if hp < 7 else None
                def na_scores(n):
                    j, hh = n // 2, n % 2
                    if j in JB:
                        jb = JB.index(j)
                        s = (hp * 4 + jb) % 2
                        if hh == 0:
                            idx = (utype * 4 + jb) * 8 + hp
                            tk.dma("sp", nstage[s][:], c_nab[idx, :, :], writes=[nstage_b[s]], track=nstage_b[s])
                            tk.op("act", lambda g: g.activation(out=eb_s[s][:].rearrange("p a b c -> p (a b c)"),
                                                                in_=nstage[s][:], func=AF.Exp),
                                  reads=[nstage_b[s]], writes=[eb_s_b[s]])
                        ebv, ebb = eb_s[s][:, hh, :, :].rearrange("p b c -> p (b c)"), eb_s_b[s]
                    else:
                        o0 = (2 * hp + hh) * 640
                        ebv, ebb = eb_res[:, o0:o0 + 640], eb_res_b
                    base = hh * 64
                    xi = n % 2
                    ti = n % 3
                    pa, pab = nextf()
                    pc2, pc2b = nextf()

                    def f(g):
                        last = None
                        for kt in range(5):
                            e = j + kt
                            dst = pa[:, kt * 128:(kt + 1) * 128] if kt < 4 else pc2[:, 0:128]
                            last = g.matmul(dst, lhsT=nkT[base:base + 64, e * 128:(e + 1) * 128],
                                            rhs=nqT[base:base + 64, j * 128:(j + 1) * 128], start=True, stop=True)
                        return last
                    mm_group([pab, pc2b], [nqT_b[j // 4]] + [nkT_b[(j + kt) // 4] for kt in range(5)], f)
                    tk.op("act", lambda g: g.activation(out=ex[xi][:, 0:512], in_=pa[:, 0:512], func=AF.Exp),
                          reads=[pab], writes=[ex_b[xi]])
                    tk.op("act", lambda g: g.activation(out=ex[xi][:, 512:640], in_=pc2[:, 0:128], func=AF.Exp),
                          reads=[pc2b], writes=[ex_b[xi]])
                    tk.op("dve", lambda g: g.tensor_tensor(out=et[ti][:], in0=ex[xi][:], in1=ebv, op=ALU.mult),
                          reads=[ex_b[xi], ebb], writes=[et_b[ti]])

                def na_pv(n):
                    j, hh = n // 2, n % 2
                    base = hh * 64
                    ti = n % 3
                    yb_, yb_b = ybb[j % 2], ybb_b[j % 2]
                    po, pob = nextf()

                    def f(g):
                        last = None
                        for kt in range(5):
                            last = g.matmul(po[:, 0:65], lhsT=et[ti][:, kt * 128:(kt + 1) * 128],
                                            rhs=nv[:, j + kt, hh, :], start=(kt == 0), stop=(kt == 4))
                        return last
                    mm_group([pob], [et_b[ti]] + [nv_b[j + kt] for kt in range(5)], f)
                    c1, c1b = nextsm()
                    tk.op("dve", lambda g: g.reciprocal(out=c1[:, 0:1], in_=po[:, 64:65]), reads=[pob],
                          writes=[c1b])
                    tk.op("dve", lambda g: g.scalar_tensor_tensor(
                        out=yb_[:, base:base + 64], in0=po[:, 0:64], scalar=c1[:, 0:1],
                        in1=nz[:, j, base:base + 64], op0=ALU.mult, op1=ALU.mult),
                        reads=[pob, c1b, nz_b[j]], writes=[yb_b])
                    if hh == 0:
                        return
                    pb, pbb = nextb()
                    mm_group([pbb], [yb_b, ident_b],
                             lambda g: g.transpose(out=pb[:, 0:128], in_=yb_[:], identity=ident[:]))
                    yi = ystr["i"] % 4
                    ystr["i"] += 1
                    tk.op("act", lambda g: g.activation(out=yst[yi][:, 0, :], in_=pb[:, 0:128], func=AF.Copy),
                          reads=[pbb], writes=[yst_b[yi]])
                    tk.dma("sp", sc_yb[u, :, hp, j * 128:(j + 1) * 128], yst[yi][:, 0, :],
                           reads=[yst_b[yi]], writes=[scyb_b[u][j]], track=yst_b[yi])

                na_scores(0)
                for n in range(32):
                    adv(bgen, 2)
                    if n + 1 < 32:
                        na_scores(n + 1)
                    na_pv(n)
                if bgen is not None:
                    run(bgen)

            rr["split"], rr["n"] = False, NPF
            barrier(nstage_b)
            ph.close()
            ph = contextlib.ExitStack()
            sbp = phase_alloc(ph)
            fnw = sbp("fnw", [128, D], F32)
            tk.dma("sp", fnw[:], c_fnw[:, :], writes=[pcst_b], track=pcst_b)
            ytl = [sbp("ytl%d" % i, [128, 8, 1024], BF16) for i in range(2)]
            mT = sbp("mT", [128, 8, 1024], BF16)
            mT_b = [Buf("mT%d" % i) for i in range(8)]
            tt_f = [sbp("ttf%d" % i, [128, 512], F32) for i in range(4)]
            tt_b = [Buf("ttf%d" % i) for i in range(4)]
            res = [sbp("res%d" % i, [128, D], F32) for i in range(2)]
            for tb in range(2):
                xdeps = xnT_b[2 + tb * 8:2 + tb * 8 + 8]
                tk.dma("sp", ytl[0][:], sc_ya[u, :, :, tb * 1024:(tb + 1) * 1024], reads=scya_b[u][tb * 8:tb * 8 + 8],
                       writes=[ytl_b[0]], track=ytl_b[0])
                tk.dma("sp", ytl[1][:], sc_yb[u, :, :, tb * 1024:(tb + 1) * 1024], reads=scyb_b[u][tb * 8:tb * 8 + 8],
                       writes=[ytl_b[1]], track=ytl_b[1])
                for f2_ in range(4):
                    wts = [load_w(w_da, f2_ * 256), load_w(w_in, GA0 + f2_ * 256),
                           load_w(w_db, f2_ * 256), load_w(w_in, GB0 + f2_ * 256)]
                    for m in range(2):
                        fo = f2_ * 2 + m
                        for hf in range(2):
                            tcols = slice(256 + tb * 1024 + hf * 512, 256 + tb * 1024 + (hf + 1) * 512)
                            outs = []
                            for br in range(2):
                                wd, wdb_ = wts[br * 2]
                                wg_, wgb_ = wts[br * 2 + 1]
                                pdn, pdnb = nextf()

                                def f(g):
                                    last = None
                                    for fc in range(8):
                                        last = g.matmul(pdn[:, 0:512], lhsT=wd[:, fc, m * 128:(m + 1) * 128],
                                                        rhs=ytl[br][:, fc, hf * 512:(hf + 1) * 512], start=(fc == 0),
                                                        stop=(fc == 7))
                                    return last
                                mm_group([pdnb], [wdb_, ytl_b[br]], f)
                                pgt, pgtb = nextf()

                                def f(g):
                                    last = None
                                    for fc in range(8):
                                        last = g.matmul(pgt[:, 0:512], lhsT=wg_[:, fc, m * 128:(m + 1) * 128],
                                                        rhs=xnT[:, fc, tcols], start=(fc == 0), stop=(fc == 7))
                                    return last
                                mm_group([pgtb], [wgb_] + xdeps, f)
                                sg, sgb = tt_f[br * 2], tt_b[br * 2]
                                pr_, prb_ = tt_f[br * 2 + 1], tt_b[br * 2 + 1]
                                tk.op("act", lambda g: g.activation(out=sg[:], in_=pgt[:, 0:512], func=AF.Tanh, scale=0.5),
                                      reads=[pgtb], writes=[sgb])
                                tk.op("dve", lambda g: g.scalar_tensor_tensor(out=pr_[:], in0=sg[:], scalar=1.0,
                                                                              in1=pdn[:, 0:512], op0=ALU.add, op1=ALU.mult),
                                      reads=[pdnb, sgb], writes=[prb_])
                                outs.append((pr_, prb_))
                            tk.op("dve", lambda g: g.tensor_tensor(out=mT[:, fo, hf * 512:(hf + 1) * 512],
                                                                   in0=outs[0][0][:], in1=outs[1][0][:],
                                                                   op=ALU.add),
                                  reads=[outs[0][1], outs[1][1]], writes=[mT_b[fo]])
                wos = [load_w(w_out, i * 256) for i in range(4)]
                for tl in range(8):
                    t = tb * 8 + tl
                    rs_i = t % 2
                    r_, r_b = res[rs_i], res_b[rs_i]
                    x_, x_b = xs[rs_i], xs_b[rs_i]
                    tk.dma("sp", x_[:], xu[u, 256 + t * 128:256 + (t + 1) * 128, :], writes=[x_b], track=x_b)
                    for half in range(2):
                        po, pob = nextf()

                        def f(g):
                            last = None
                            for q4 in range(2):
                                wo_, _ = wos[half * 2 + q4]
                                for fc in range(8):
                                    last = g.matmul(po[:, q4 * 256:(q4 + 1) * 256],
                                                    lhsT=mT[:, fc, tl * 128:(tl + 1) * 128], rhs=wo_[:, fc, :],
                                                    start=(fc == 0), stop=(fc == 7))
                            return last
                        mm_group([pob], mT_b + [wos[half * 2][1], wos[half * 2 + 1][1]], f)
                        tk.op("dve", lambda g: g.scalar_tensor_tensor(out=r_[:, half * 512:(half + 1) * 512],
                                                                      in0=po[:, 0:512], scalar=0.5,
                                                                      in1=x_[:, half * 512:(half + 1) * 512],
                                                                      op0=ALU.mult, op1=ALU.add),
                              reads=[pob, x_b], writes=[r_b])
                    c1, c1b = nextsm()
                    tk.op("act", lambda g: g.activation(out=junk[:], in_=r_[:], func=AF.Square, accum_out=c1[:, 0:1]),
                          reads=[r_b], writes=[junk_b, c1b])
                    tk.op("dve", lambda g: g.tensor_scalar(out=c1[:, 1:2], in0=c1[:, 0:1], scalar1=1.0 / D,
                                                           scalar2=EPS, op0=ALU.mult, op1=ALU.add),
                          reads=[c1b], writes=[c1b])
                    tk.op("pool", lambda g: g.tensor_tensor(out=c1[:, 2:3], in0=c1[:, 1:2], in1=col[:, 1:2],
                                                            op=ALU.pow), reads=[c1b, col_b], writes=[c1b])
                    tk.op("dve", lambda g: g.scalar_tensor_tensor(out=r_[:], in0=r_[:], scalar=c1[:, 2:3],
                                                                  in1=fnw[:, :], op0=ALU.mult, op1=ALU.mult),
                          reads=[r_b, c1b, pcst_b], writes=[r_b])
                    tk.dma("sp", yu[u, t * 128:(t + 1) * 128, :], r_[:], reads=[r_b], track=r_b)

            barrier(res_b + ytl_b + yst_b + [pcst_b])
            ph.close()

        tk.wait_all("sp", res_b + yst_b + dbg_b)
    return nc


def _na_tables(rpb, tc, bc, js):
    H = rpb.shape[0]
    out = np.full((len(js), H, 128, 5, 128), NEGB, np.float32)
    qc = np.arange(64)
    kc = np.arange(64)
    cs = np.clip(qc - 8, 0, 48)
    cvalid = (kc[:, None] >= cs[None, :]) & (kc[:, None] < cs[None, :] + 16)
    dci = np.clip(kc[:, None] - qc[None, :] + 15, 0, 30)
    for ji, j in enumerate(js):
        for kt in range(5):
            for a in range(2):
                s = 2 * (j + kt) - 4 + a
                if s < 0:
                    kr = 8 + s if tc else s
                    dup = tc and (kr <= 2 * j + 5)
                elif s >= 32:
                    if s == 35:
                        continue
                    kr = s - 8 if bc else s
                    dup = bc and (kr >= 2 * j - 4)
                else:
                    kr = s
                    dup = False
                if dup:
                    continue
                for b in range(2):
                    r = 2 * j + b
                    lo = r - 4
                    if tc:
                        lo = max(lo, 0)
                    if bc:
                        lo = min(lo, 24)
                    if not (lo <= kr <= lo + 7):
                        continue
                    dr = kr - r
                    blk = rpb[:, dr + 7, :][:, dci]
                    blk = np.where(cvalid[None], blk, np.float32(NEGB))
                    out[ji, :, a * 64:(a + 1) * 64, kt, b * 64:(b + 1) * 64] = blk
    return out


def _unit_x(seq, top, bot, conv4):
    blk = np.zeros((XROWS, D), np.float32)
    blk[0:256] = top
    blk[256:256 + UT] = seq
    blk[2304:2304 + 192] = bot
    blk[2560:2564] = conv4
    return blk


def _prep_inputs(inputs):
    f = lambda a: np.ascontiguousarray(np.asarray(a), dtype=np.float32)
    xp = f(inputs["x_prompt"])[0]
    xsm = f(inputs["x_sample"])
    w_in = f(inputs["w_in"])[0]
    rpb = f(inputs["rpb"])[0]
    conv_w = f(inputs["conv_w"])[0]
    conv_b = f(inputs["conv_b"])[0]
    bgate = f(inputs["b_gate"])[0]
    common = {
        "w_in": w_in,
        "w_da": f(inputs["w_down_a"])[0],
        "w_db": f(inputs["w_down_b"])[0],
        "w_out": f(inputs["w_out"])[0],
        "c_normT": np.ascontiguousarray(f(inputs["norm_w"])[0].reshape(8, 128).T),
        "c_normb": np.ascontiguousarray(np.broadcast_to(f(inputs["norm_w"])[0][None, :], (128, D))),
        "c_cw": np.ascontiguousarray(conv_w.reshape(5, 16, 128).transpose(2, 1, 0).reshape(128, 80)),
        "c_cb": np.ascontiguousarray(conv_b.reshape(16, 128).T),
        "c_bg": np.ascontiguousarray(np.broadcast_to(f(inputs["b_gate"])[0][None, :], (128, 16))),
        "c_mhw": np.ascontiguousarray(np.broadcast_to(f(inputs["mh_norm_w"])[0][None, :], (128, D))),
        "c_fnw": np.ascontiguousarray(np.broadcast_to(f(inputs["final_norm_w"])[None, :], (128, D))),
        "c_ident": np.eye(128, dtype=np.float32),
    }
    s_ = np.arange(128)
    mf = (s_[:, None] <= s_[None, :]).astype(np.float32)
    mb = (s_[:, None] >= s_[None, :]).astype(np.float32)
    common["c_mask"] = np.ascontiguousarray(np.concatenate([mf, mb, np.ones((128, 128), np.float32)], axis=1))
    nai = _na_tables(rpb, False, False, [5])[0]
    common["c_nai"] = np.ascontiguousarray(nai.transpose(1, 0, 2, 3).reshape(128, 16 * 5 * 128))

    def nab_for(tc, bc):
        t = _na_tables(rpb, tc, bc, list(JB))
        t = t.reshape(4, 8, 2, 128, 5, 128).transpose(0, 1, 3, 2, 4, 5)
        return t.reshape(4 * 8, 128, 1280)
    nab_sample = nab_for(True, True)
    z4 = np.zeros((4, D), np.float32)
    in_maps = []
    for c in range(NCORES):
        xu = np.zeros((NUNITS, XROWS, D), np.float32)
        for i in range(4):
            sq = xsm[4 * c + i]
            xu[i] = _unit_x(sq, sq[256:512], sq[24 * 64:27 * 64], z4)
        t0 = c * UT
        seg = xp[t0:t0 + UT]
        top = xp[256:512] if c == 0 else xp[t0 - 256:t0]
        bot = xp[248 * 64:251 * 64] if c == NCORES - 1 else xp[t0 + UT:t0 + UT + 192]
        cv = z4.copy()
        if c > 0:
            cv[0:2] = xp[t0 - 2:t0]
        if c < NCORES - 1:
            cv[2:4] = xp[t0 + UT:t0 + UT + 2]
        xu[4] = _unit_x(seg, top, bot, cv)
        nab = np.concatenate([nab_sample, nab_for(c == 0, c == NCORES - 1)], axis=0)
        m = dict(common)
        m["xu"] = xu
        m["c_nab"] = np.ascontiguousarray(nab)
        xpre = np.zeros((7, 17 * 128, D), np.float32)
        pgw = np.zeros((7, D, 8), np.float32)
        pgb = np.zeros((128, 7, 8), np.float32)
        pcw = np.zeros((128, 7, 8, 5), np.float32)
        flags = np.zeros((128, 16), np.float32)
        for i in range(7):
            if i < c:
                sg, flip = i, False
            else:
                sg, flip = 7 - (i - c), True
            s0 = sg * UT
            rows = xp[s0:s0 + UT]
            halo = np.zeros((4, D), np.float32)
            if not flip:
                if s0 >= 2:
                    halo[0:2] = xp[s0 - 2:s0]
                if s0 + UT + 2 <= 16384:
                    halo[2:4] = xp[s0 + UT:s0 + UT + 2]
                gi, gf = slice(MG0, MG0 + 4), slice(MG0 + 4, MG0 + 8)
                taps = [0, 1, 2, 3, 4]
            else:
                rows = rows[::-1]
                if s0 + UT + 2 <= 16384:
                    halo[0] = xp[s0 + UT + 1]
                    halo[1] = xp[s0 + UT]
                if s0 >= 2:
                    halo[2] = xp[s0 - 1]
                    halo[3] = xp[s0 - 2]
                gi, gf = slice(MG0 + 8, MG0 + 12), slice(MG0 + 12, MG0 + 16)
                taps = [4, 3, 2, 1, 0]
            xpre[i, 0:UT] = rows
            xpre[i, UT:UT + 4] = halo
            pgw[i, :, 0:4] = w_in[:, gi]
            pgw[i, :, 4:8] = w_in[:, gf]
            pgb[:, i, 0:4] = bgate[gi.start - MG0:gi.stop - MG0][None, :]
            pgb[:, i, 4:8] = bgate[gf.start - MG0:gf.stop - MG0][None, :]
            pcw[:, i] = conv_w[taps][:, 1024:2048].reshape(5, 8, 128).transpose(2, 1, 0)
            flags[:, i] = 0.0 if i == c else 1.0
            flags[:, 7 + i] = 1.0 if i == c - 1 else 0.0
        flags[:, 14] = 1.0 if c < NCORES - 1 else 0.0
        m["xpre"] = xpre
        m["c_pgw"] = pgw
        m["c_pgb"] = np.ascontiguousarray(pgb.reshape(128, 56))
        m["c_pcw"] = np.ascontiguousarray(pcw.reshape(128, 280))
        m["c_pflags"] = flags
        in_maps.append(m)
    return in_maps


_PROGRAM = {}


def kernel(**inputs):
    in_maps = _prep_inputs(inputs)
    if "nc" not in _PROGRAM:
        _PROGRAM["nc"] = build_program()
    nc = _PROGRAM["nc"]
    res = run_bass_kernel_spmd(nc, in_maps, core_ids=list(range(NCORES)))
    y_prompt = np.zeros((1, 16384, D), np.float32)
    y_sample = np.zeros((32, UT, D), np.float32)
    for c in range(NCORES):
        yu = np.asarray(res.results[c]["yu"]).reshape(NUNITS, UT, D)
        for i in range(4):
            y_sample[4 * c + i] = yu[i]
        y_prompt[0, c * UT:(c + 1) * UT] = yu[4]
    return (y_prompt, y_sample)
```
